# Optimizing a Trainium2 kernel written in Bass

```python
import jax, jax.numpy as jnp
from jax import lax
import numpy as np

D_MODEL = 1024
BATCH = 32
SEQ = 2048
DEPTH = 4

CHUNK = 64
EPS = 1e-6
Q_BLOCK = 128

POOL_WINDOWS = (2, 4, 8, 16)
POOL_GROUPS = len(POOL_WINDOWS)
POOL_GROUP_DIM = D_MODEL // 16
POOL_WIDTH = POOL_GROUPS * POOL_GROUP_DIM
SB_HEADS = 8
SB_HEAD_DIM = D_MODEL // 16
SB_WIDTH = SB_HEADS * SB_HEAD_DIM
CONV_K = 3
CONV_WIDTH = D_MODEL // 4
N_BRANCH = 3
D_FF = ((8 * D_MODEL // 3 + 255) // 256) * 256

OFF_POOL = 0
OFF_Q = OFF_POOL + POOL_WIDTH
OFF_K = OFF_Q + SB_WIDTH
OFF_V = OFF_K + SB_WIDTH
OFF_CX = OFF_V + SB_WIDTH
OFF_CB = OFF_CX + CONV_WIDTH
OFF_CC = OFF_CB + CONV_WIDTH
OFF_GATE = OFF_CC + CONV_WIDTH
IN_WIDTH = OFF_GATE + N_BRANCH * D_MODEL

kernel_name = "hybrid_pool_stickbreak_shortconv_macaron"


def _rmsnorm(x, g):
    x32 = x.astype(jnp.float32)
    y = x32 * lax.rsqrt(jnp.mean(x32 * x32, axis=-1, keepdims=True) + EPS)
    return (y * g.astype(jnp.float32)).astype(x.dtype)


def _swiglu(h, w_gate, w_up, w_down):
    return (jax.nn.silu(h @ w_gate) * (h @ w_up)) @ w_down


def _pool_mixer(u, w_mix, scale):
    b, s, _ = u.shape
    ug = u.reshape(b, s, POOL_GROUPS, POOL_GROUP_DIM)
    count = jnp.arange(1, s + 1, dtype=jnp.float32)
    outs = []
    for g, w in enumerate(POOL_WINDOWS):
        ui = ug[:, :, g].astype(jnp.float32)
        cs = jnp.cumsum(ui, axis=1)
        lag = jnp.pad(cs, ((0, 0), (w, 0), (0, 0)))[:, :s]
        mean = (cs - lag) / jnp.minimum(count, float(w))[None, :, None]
        outs.append(mean - ui)
    pooled = jnp.stack(outs, axis=2).astype(u.dtype)
    mixed = jnp.einsum('bsgc,gcd->bsgd', pooled, w_mix).reshape(b, s, POOL_WIDTH)
    return mixed * scale


def _stick_breaking(q, k, v):
    s = q.shape[1]
    scale = SB_HEAD_DIM ** -0.5
    outs = []
    for i in range(s // Q_BLOCK):
        q0 = i * Q_BLOCK
        klen = q0 + Q_BLOCK
        qb = q[:, q0:klen]
        kb = k[:, :klen]
        vb = v[:, :klen]
        z = jnp.einsum('bqhd,bkhd->bhqk', qb, kb).astype(jnp.float32) * scale
        qpos = q0 + jnp.arange(Q_BLOCK)
        kpos = jnp.arange(klen)
        mask = kpos[None, :] < qpos[:, None]
        log_beta = jax.nn.log_sigmoid(z)
        log_keep = jnp.where(mask, jax.nn.log_sigmoid(-z), 0.0)
        between = lax.cumsum(log_keep, axis=3, reverse=True) - log_keep
        wts = jnp.where(mask, jnp.exp(log_beta + between), 0.0)
        outs.append(jnp.einsum('bhqk,bkhd->bqhd', wts.astype(v.dtype), vb))
    return jnp.concatenate(outs, axis=1)


def _short_conv(xc, gb, gc, conv_w, conv_b):
    s = xc.shape[1]
    u = gc * xc
    up = jnp.pad(u, ((0, 0), (CONV_K - 1, 0), (0, 0)))
    y = conv_b + sum(conv_w[j] * up[:, j:j + s] for j in range(CONV_K))
    return gb * y


def _mixing(h, w_in, b_gate, pool_w, pool_scale, conv_w, conv_b,
            w_br_pool, w_br_sb, w_br_conv, w_out):
    b, s, _ = h.shape
    p = h @ w_in
    a_out = _pool_mixer(p[..., OFF_POOL:OFF_Q], pool_w, pool_scale)
    q = p[..., OFF_Q:OFF_K].reshape(b, s, SB_HEADS, SB_HEAD_DIM)
    k = p[..., OFF_K:OFF_V].reshape(b, s, SB_HEADS, SB_HEAD_DIM)
    v = p[..., OFF_V:OFF_CX].reshape(b, s, SB_HEADS, SB_HEAD_DIM)
    b_out = _stick_breaking(q, k, v).reshape(b, s, SB_WIDTH)
    c_out = _short_conv(p[..., OFF_CX:OFF_CB], p[..., OFF_CB:OFF_CC], p[..., OFF_CC:OFF_GATE],
                        conv_w, conv_b)
    gates = jax.nn.sigmoid(p[..., OFF_GATE:].reshape(b, s, N_BRANCH, D_MODEL) + b_gate)
    merged = (gates[:, :, 0] * (a_out @ w_br_pool)
              + gates[:, :, 1] * (b_out @ w_br_sb)
              + gates[:, :, 2] * (c_out @ w_br_conv))
    return merged @ w_out


def setup_inputs(seed: int = 0) -> dict:
    key = jax.random.key(seed)
    ks = iter(jax.random.split(key, 32))
    f32 = jnp.float32

    def nrm(shape, fan_in):
        return jax.random.normal(next(ks), shape, f32) * (fan_in ** -0.5)

    def gain(shape):
        return 1.0 + 0.05 * jax.random.normal(next(ks), shape, f32)

    def small(shape):
        return 0.01 * jax.random.normal(next(ks), shape, f32)

    L, D = DEPTH, D_MODEL
    return {
        "x": jax.random.normal(next(ks), (BATCH, SEQ, D), f32),
        "ffn1_pre_g": gain((L, D)),
        "ffn1_post_g": gain((L, D)),
        "ffn1_w_gate": nrm((L, D, D_FF), D),
        "ffn1_w_up": nrm((L, D, D_FF), D),
        "ffn1_w_down": nrm((L, D_FF, D), D_FF),
        "mix_pre_g": gain((L, D)),
        "mix_post_g": gain((L, D)),
        "w_in": nrm((L, D, IN_WIDTH), D),
        "b_gate": small((L, N_BRANCH, D)),
        "pool_w": nrm((L, POOL_GROUPS, POOL_GROUP_DIM, POOL_GROUP_DIM), POOL_GROUP_DIM),
        "pool_scale": gain((L, POOL_WIDTH)),
        "conv_w": nrm((L, CONV_K, CONV_WIDTH), CONV_K),
        "conv_b": small((L, CONV_WIDTH)),
        "w_br_pool": nrm((L, POOL_WIDTH, D), POOL_WIDTH),
        "w_br_sb": nrm((L, SB_WIDTH, D), SB_WIDTH),
        "w_br_conv": nrm((L, CONV_WIDTH, D), CONV_WIDTH),
        "w_out": nrm((L, D, D), D),
        "ffn2_pre_g": gain((L, D)),
        "ffn2_post_g": gain((L, D)),
        "ffn2_w_gate": nrm((L, D, D_FF), D),
        "ffn2_w_up": nrm((L, D, D_FF), D),
        "ffn2_w_down": nrm((L, D_FF, D), D_FF),
    }


def reference(x, ffn1_pre_g, ffn1_post_g, ffn1_w_gate, ffn1_w_up, ffn1_w_down,
              mix_pre_g, mix_post_g, w_in, b_gate, pool_w, pool_scale, conv_w, conv_b,
              w_br_pool, w_br_sb, w_br_conv, w_out,
              ffn2_pre_g, ffn2_post_g, ffn2_w_gate, ffn2_w_up, ffn2_w_down):
    for l in range(DEPTH):
        h = _swiglu(_rmsnorm(x, ffn1_pre_g[l]), ffn1_w_gate[l], ffn1_w_up[l], ffn1_w_down[l])
        x = x + 0.5 * _rmsnorm(h, ffn1_post_g[l])
        h = _mixing(_rmsnorm(x, mix_pre_g[l]), w_in[l], b_gate[l], pool_w[l], pool_scale[l],
                    conv_w[l], conv_b[l], w_br_pool[l], w_br_sb[l], w_br_conv[l], w_out[l])
        x = x + _rmsnorm(h, mix_post_g[l])
        h = _swiglu(_rmsnorm(x, ffn2_pre_g[l]), ffn2_w_gate[l], ffn2_w_up[l], ffn2_w_down[l])
        x = x + 0.5 * _rmsnorm(h, ffn2_post_g[l])
    return x
```

```python
import numpy as np
from contextlib import ExitStack
import concourse.bass as bass
import concourse.mybir as mybir
from concourse.bass_utils import run_bass_kernel_spmd

F32 = mybir.dt.float32
BF16 = mybir.dt.bfloat16
AF = mybir.ActivationFunctionType
ALU = mybir.AluOpType

L = 4
D = 1024
S = 2048
T = 512
NT = 4
DC = 8
FC = 22
NCORE = 8
SEQ_PER_CORE = 4
EPS = 1e-6
NV = 362
V_G, V_BG, V_PS, V_CW, V_CB, V_IW, V_IC = 0, 192, 288, 296, 320, 328, 330

EPOCH = 30000


class Op:
    __slots__ = ("eng", "fn", "reads", "writes", "dma_key", "seq", "waits", "signal", "rank", "clock", "res")

    def __init__(self, eng, fn, reads, writes, dma_key):
        self.eng = eng
        self.fn = fn
        self.reads = reads
        self.writes = writes
        self.dma_key = dma_key
        self.seq = 0
        self.waits = []
        self.signal = False
        self.rank = 0
        self.clock = None
        self.res = None


class Prog:
    ENGS = ("pe", "act", "dve", "pool", "sp")

    def __init__(self):
        self.ops = []

    def add(self, eng, fn, reads=(), writes=(), dma_key=None):
        reads = tuple(reads)
        writes = tuple(writes) + tuple(k for k in reads if isinstance(k, tuple) and k[0] == "ps")
        self.ops.append(Op(eng, fn, reads, writes, dma_key))

    def analyze(self):
        last_writer = {}
        readers = {}
        seqctr = {}
        known = {e: {} for e in self.ENGS}
        by_res = {}
        for op in self.ops:
            res = ("dma:" + str(op.dma_key)) if op.dma_key is not None else op.eng
            seqctr[res] = seqctr.get(res, 0) + 1
            op.seq = seqctr[res]
            op.res = res
            by_res[(res, op.seq)] = op
            deps = set()
            for r in op.reads:
                w = last_writer.get(r)
                if w is not None:
                    deps.add(w)
            for w_ in op.writes:
                w = last_writer.get(w_)
                if w is not None:
                    deps.add(w)
                for rd in readers.get(w_, ()):
                    deps.add(rd)
            deps.discard(op)
            kn = known[op.eng]
            need = {}
            for d in deps:
                if d.res == op.eng and op.dma_key is None and op.eng == "pe":
                    continue
                if kn.get(d.res, 0) >= d.seq:
                    continue
                if need.get(d.res, 0) < d.seq:
                    need[d.res] = d.seq
            items = sorted(need.items())
            final = []
            for r, v in items:
                implied = False
                for r2, v2 in items:
                    if (r2, v2) == (r, v):
                        continue
                    if by_res[(r2, v2)].clock.get(r, 0) >= v:
                        implied = True
                        break
                if not implied:
                    final.append((r, v))
            op.waits = final
            for r, v in final:
                d = by_res[(r, v)]
                d.signal = True
                for rr, vv in d.clock.items():
                    if kn.get(rr, 0) < vv:
                        kn[rr] = vv
            ck = dict(kn)
            ck[res] = op.seq
            op.clock = ck
            for r in op.reads:
                readers.setdefault(r, []).append(op)
            for w_ in op.writes:
                last_writer[w_] = op
                readers[w_] = []
        rk = {}
        for op in self.ops:
            if op.dma_key is None:
                if op.signal:
                    rk[op.res] = rk.get(op.res, 0) + 1
                    op.rank = rk[op.res]
            else:
                op.rank = op.seq
        self.nsig = rk
        self.by_res = by_res

    def emit(self, nc, stack):
        self.analyze()
        sems = {}
        for e in self.ENGS:
            n = self.nsig.get(e, 0)
            for k in range((n + EPOCH - 1) // EPOCH):
                sems[(e, k)] = stack.enter_context(nc.semaphore("s_%s_%d" % (e, k)))
        dkeys = sorted({op.res for op in self.ops if op.dma_key is not None})
        for i, r in enumerate(dkeys):
            sems[(r, 0)] = stack.enter_context(nc.semaphore("sd_%d" % i))
        by_res = self.by_res

        def sem_val(r, v):
            d = by_res[(r, v)]
            if d.dma_key is not None:
                return sems[(r, 0)], 16 * d.rank
            k = (d.rank - 1) // EPOCH
            return sems[(r, k)], (d.rank - 1) % EPOCH + 1

        per_eng = {e: [] for e in self.ENGS}
        for op in self.ops:
            per_eng[op.eng].append(op)
        block = stack.enter_context(nc.Block())

        def run(e, ops):
            def body(eng):
                for op in ops:
                    for r, v in op.waits:
                        s, val = sem_val(r, v)
                        eng.wait_ge(s, val)
                    ins = op.fn(eng)
                    if ins is None:
                        continue
                    if op.dma_key is not None:
                        ins.then_inc(sems[(op.res, 0)], 16)
                    elif op.signal:
                        k = (op.rank - 1) // EPOCH
                        ins.then_inc(sems[(e, k)], 1)
            return body

        block.tensor(run("pe", per_eng["pe"]))
        block.scalar(run("act", per_eng["act"]))
        block.vector(run("dve", per_eng["dve"]))
        block.gpsimd(run("pool", per_eng["pool"]))
        block.sync(run("sp", per_eng["sp"]))


class Rot:
    def __init__(self, items):
        self.items = list(items)
        self.i = 0

    def next(self):
        v = self.items[self.i % len(self.items)]
        self.i += 1
        return v


def build_program(NS=SEQ_PER_CORE, NL=L, parts=("ffn1", "mix", "ffn2")):
    nc = bass.Bass("TRN2", target_bir_lowering=False)

    def din(name, shape, dt=F32):
        return nc.dram_tensor(name, list(shape), dt, kind="ExternalInput").ap()

    def dscr(name, shape):
        return nc.dram_tensor(name, list(shape), BF16, kind="Internal").ap()

    xT = din("xT", [NS, DC, 128, S])
    yT = nc.dram_tensor("yT", [NS, DC, 128, S], F32, kind="ExternalOutput").ap()
    wshapes = {
        "wgu": [L, 2, FC, 128, 16 * 128],
        "wd": [L, 2, DC, 2, 128, 11 * 128],
        "win": [L, 8, 128, 16 * 128],
        "wv": [L, 2, 128, 4 * 512],
        "wgb": [L, DC, 2, 128, 16 * 128],
        "wo": [L, 4, 128, 16 * 128],
    }
    wf = {k: din(k, v) for k, v in wshapes.items()}
    wb = {k: dscr(k + "_b", v) for k, v in wshapes.items()}
    vec_d = din("vec", [128, NV])
    cst_d = din("cst", [128, 256])
    pw_d = din("poolw", [128, L * 2 * 128])

    P = Prog()
    with ExitStack() as st:
        def sb(name, shape, dt):
            return st.enter_context(nc.sbuf_tensor(name, list(shape), dt))

        X = sb("X", [128, DC, S], F32)
        HN = sb("HN", [128, DC, 2 * T], BF16)
        RA = sb("RA", [128, FC * 2 * T], BF16)
        H = sb("H", [128, DC, T], F32)
        PM = [sb("PM%d" % i, [128, 528], F32) for i in range(3)]
        PH = sb("PH", [128, 2, 16], F32)
        CH = sb("CH", [128, 2, 2], F32)
        PD = sb("PD", [128, T], BF16)
        AO = sb("AO", [128, 2, T], BF16)
        BO = sb("BO", [128, 4, T], BF16)
        CO = sb("CO", [128, 2, T], BF16)
        SG = [sb("SG%d" % i, [128, T], F32) for i in range(4)]
        AE = [sb("AE%d" % i, [128, T], F32) for i in range(2)]
        ASP = [sb("ASP%d" % i, [128, T], BF16) for i in range(2)]
        AW = [sb("AW%d" % i, [128, T], BF16) for i in range(2)]
        R32 = [sb("R32_%d" % i, [128, T], F32) for i in range(2)]
        RB = [sb("RB%d" % i, [128, T], BF16) for i in range(2)]
        WB = [sb("WB%d" % i, [128, 2048], BF16) for i in range(4)]
        CB = sb("CB", [128, 256], BF16)
        PW = sb("PW", [128, L * 2 * 128], BF16)
        ONESM = sb("ONESM", [128, 128], BF16)
        ONES1 = sb("ONES1", [128, 128], BF16)
        VEC = sb("VEC", [128, NV], F32)
        RSTD = [sb("RSTD%d" % i, [128, T], F32) for i in range(2)]
        SQ = [sb("SQ%d" % i, [128, T], BF16) for i in range(2)]
        DUM = sb("DUM", [128, 2], F32)
        ps = [st.enter_context(nc.psum_tensor("ps%d" % i, [128, T], F32)) for i in range(8)]

        TRI = CB[:, 0:128]
        MSK = CB[:, 128:256]

        def A_(m, tt):
            return RA[:, m * 1024 + tt * 512: m * 1024 + (tt + 1) * 512]

        def KT_(c, a, b):
            return RA[:, c * S + a: c * S + b]

        def V_(kb):
            return RA[:, 8192 + kb * 512: 8192 + (kb + 1) * 512]

        def QS_(c):
            return RA[:, 16384 + c * 512: 16384 + (c + 1) * 512]

        def NQ_(c):
            return RA[:, 18432 + c * 512: 18432 + (c + 1) * 512]

        A_KEYS = [("A", m, tt) for m in range(FC) for tt in range(2)]
        M_KEYS = ([("KT", c, t) for c in range(4) for t in range(NT)] + [("V", kb) for kb in range(16)]
                  + [("QS", c) for c in range(4)] + [("NQ", c) for c in range(4)])

        def vcol(i):
            return VEC[:, i:i + 1]

        rot_rstd = Rot([0, 1])
        rot_sq = Rot([0, 1])
        rot_wb = Rot([0, 1, 2, 3])
        rot_sg = Rot([0, 1, 2, 3])

        P.add("sp", lambda e: e.dma_start(out=VEC[:], in_=vec_d), writes=["VEC"], dma_key="VEC")
        Hflat = H[:, :, :]
        P.add("sp", lambda e: e.dma_start(out=H[:, 0, 0:256], in_=cst_d), writes=[("H", 0)], dma_key="H0")
        P.add("sp", lambda e: e.dma_start(out=H[:, 2, :], in_=pw_d[:, 0:512]), writes=[("H", 2)], dma_key="H2")
        P.add("sp", lambda e: e.dma_start(out=H[:, 3, :], in_=pw_d[:, 512:1024]), writes=[("H", 3)], dma_key="H3")
        P.add("dve", lambda e: e.tensor_copy(out=CB[:], in_=H[:, 0, 0:256]), reads=[("H", 0)], writes=["CB"])
        P.add("dve", lambda e: e.tensor_copy(out=PW[:, 0:512], in_=H[:, 2, :]), reads=[("H", 2)], writes=["PW"])
        P.add("dve", lambda e: e.tensor_copy(out=PW[:, 512:1024], in_=H[:, 3, :]), reads=[("H", 3)], writes=["PW"])
        P.add("dve", lambda e: e.memset(ONESM[:], 1.0 / D), writes=["ONESM"])
        P.add("dve", lambda e: e.memset(ONES1[:], 1.0), writes=["ONES1"])
        cvi = [0]

        WKEYS = []

        def convert(name, idx):
            WKEYS.append(("W", name) + tuple(idx))
            src = wf[name]
            dst = wb[name]
            for i in idx:
                src = src[i]
                dst = dst[i]
            k = cvi[0]
            cvi[0] += 1
            P.add("pool", lambda e, s_=src, d_=dst: e.dma_start(out=d_, in_=s_),
                  writes=[("W", name) + tuple(idx)], dma_key="cv%d" % k)

        for l in range(NL):
            convert("wgu", (l, 0))
            convert("wd", (l, 0))
            convert("win", (l,))
            convert("wv", (l,))
            convert("wgb", (l,))
            convert("wo", (l,))
            convert("wgu", (l, 1))
            convert("wd", (l, 1))

        def wload(src, ncols, rkey):
            s = rot_wb.next()
            P.add("sp", lambda e, s=s, src=src: e.dma_start(out=WB[s][:, 0:ncols], in_=src),
                  reads=[rkey], writes=[("WB", s)], dma_key="WB%d" % s)
            return s

        def rstd_finish(r):
            P.add("act", lambda e: e.activation(out=RSTD[r][:], in_=ps[6][:], func=AF.Ln, bias=EPS),
                  reads=[("ps", 6)], writes=[("RSTD", r)])
            P.add("act", lambda e: e.activation(out=RSTD[r][:], in_=RSTD[r][:], func=AF.Exp, scale=-0.5),
                  reads=[("RSTD", r)], writes=[("RSTD", r)])

        def prenorm(gcol, tile, hh):
            a, b = tile * T, (tile + 1) * T
            r = rot_rstd.next()
            for c in range(DC):
                q = rot_sq.next()
                P.add("act", lambda e, c=c, q=q: e.activation(out=SQ[q][:], in_=X[:, c, a:b], func=AF.Square),
                      reads=[("X", c, tile)], writes=[("SQ", q)])
                P.add("pe", lambda e, c=c, q=q: e.matmul(ps[6][:], lhsT=ONESM[:], rhs=SQ[q][:], start=(c == 0),
                                                         stop=(c == DC - 1)),
                      reads=[("SQ", q), "ONESM"], writes=[("ps", 6)])
            rstd_finish(r)
            for c in range(DC):
                P.add("dve", lambda e, c=c: e.scalar_tensor_tensor(
                    out=HN[:, c, hh * T:(hh + 1) * T], in0=X[:, c, a:b], scalar=vcol(gcol + c), in1=RSTD[r][:],
                    op0=ALU.mult, op1=ALU.mult),
                    reads=[("X", c, tile), ("RSTD", r), "VEC"], writes=[("HN", c, hh)])

        def postnorm_update(gcol, tile, factor):
            a, b = tile * T, (tile + 1) * T
            r = rot_rstd.next()
            rstd_finish(r)
            pend = None
            for c in range(DC):
                k = rot_sg.next()
                P.add("dve", lambda e, c=c, k=k: e.scalar_tensor_tensor(
                    out=SG[k][:], in0=H[:, c, :], scalar=vcol(gcol + c), in1=RSTD[r][:], op0=ALU.mult, op1=ALU.mult),
                    reads=[("H", c), ("RSTD", r), "VEC"], writes=[("SG", k)])

                def upd(c=c, k=k):
                    P.add("dve", lambda e: e.scalar_tensor_tensor(
                        out=X[:, c, a:b], in0=SG[k][:], scalar=float(factor), in1=X[:, c, a:b], op0=ALU.mult,
                        op1=ALU.add),
                        reads=[("SG", k), ("X", c, tile)], writes=[("X", c, tile)])
                if pend is not None:
                    pend()
                pend = upd
            pend()

        def evac_h_and_stats(bank, n, pending):
            q = rot_sq.next()
            P.add("dve", lambda e: e.tensor_copy(out=H[:, n, :], in_=ps[bank][:]), reads=[("ps", bank)],
                  writes=[("H", n)])
            P.add("act", lambda e: e.activation(out=SQ[q][:], in_=H[:, n, :], func=AF.Square),
                  reads=[("H", n)], writes=[("SQ", q)])

            def stat():
                P.add("pe", lambda e: e.matmul(ps[6][:], lhsT=ONESM[:], rhs=SQ[q][:], start=(n == 0),
                                               stop=(n == DC - 1)),
                      reads=[("SQ", q), "ONESM"], writes=[("ps", 6)])
            return stat

        rot_g = Rot([0, 1])
        rot_u = Rot([2, 3])
        rot_d = Rot([4, 5])

        def ffn(l, f, hf):
            gpre = V_G + ((0 if f == 0 else 4) * L + l) * 8
            gpost = V_G + ((1 if f == 0 else 5) * L + l) * 8
            for tt in range(2):
                prenorm(gpre, 2 * hf + tt, tt)
            for m in range((FC if "mlim" not in parts else 2) if "nogu" not in parts else 0):
                s = wload(wb["wgu"][l, f, m], 2048, ("W", "wgu", l, f))
                for tt in range(2):
                    g = rot_g.next()
                    u = rot_u.next()
                    for kc in range(DC):
                        P.add("pe", lambda e, kc=kc, s=s, g=g, tt=tt: e.matmul(
                            ps[g][:], lhsT=WB[s][:, kc * 128:(kc + 1) * 128], rhs=HN[:, kc, tt * T:(tt + 1) * T],
                            start=(kc == 0), stop=(kc == DC - 1)),
                            reads=[("WB", s), ("HN", kc, tt)], writes=[("ps", g)])
                    for kc in range(DC):
                        P.add("pe", lambda e, kc=kc, s=s, u=u, tt=tt: e.matmul(
                            ps[u][:], lhsT=WB[s][:, (8 + kc) * 128:(9 + kc) * 128], rhs=HN[:, kc, tt * T:(tt + 1) * T],
                            start=(kc == 0), stop=(kc == DC - 1)),
                            reads=[("WB", s), ("HN", kc, tt)], writes=[("ps", u)])
                    k = rot_sg.next()
                    P.add("act", lambda e, g=g, k=k: e.activation(out=SG[k][:], in_=ps[g][:], func=AF.Silu),
                          reads=[("ps", g)], writes=[("SG", k)])
                    P.add("dve", lambda e, u=u, k=k, m=m, tt=tt: e.tensor_tensor(
                        out=A_(m, tt), in0=ps[u][:], in1=SG[k][:], op=ALU.mult),
                        reads=[("ps", u), ("SG", k)], writes=[("A", m, tt)])
            for tt in range(2 if "nodown" not in parts else 0):
                pend = None
                for n in range(DC):
                    s0 = wload(wb["wd"][l, f, n, 0], 1408, ("W", "wd", l, f))
                    s1 = wload(wb["wd"][l, f, n, 1], 1408, ("W", "wd", l, f))
                    d = rot_d.next()
                    for kc in range(FC):
                        sl = s0 if kc < 11 else s1
                        P.add("pe", lambda e, kc=kc, sl=sl, d=d, tt=tt: e.matmul(
                            ps[d][:], lhsT=WB[sl][:, (kc % 11) * 128:(kc % 11 + 1) * 128], rhs=A_(kc, tt),
                            start=(kc == 0), stop=(kc == FC - 1)),
                            reads=[("WB", sl), ("A", kc, tt)], writes=[("ps", d)])
                    if pend is not None:
                        pend()
                    pend = evac_h_and_stats(d, n, None)
                pend()
                postnorm_update(gpost, 2 * hf + tt, 0.5)

        def fence():
            P.add("dve", lambda e: e.memset(DUM[:], 0.0), writes=A_KEYS + M_KEYS + ["DUM"])

        rot_p = Rot([0, 1, 2, 3, 4, 5])
        rot_z = Rot([0, 1])
        rot_c = Rot([2, 3])
        rot_o = Rot([4, 5])
        rot_at = Rot([0, 1])
        rot_r = Rot([0, 1])

        def mix(l, t):
            hh = t % 2
            mh = 1 - hh
            a, b = t * T, (t + 1) * T
            prenorm(V_G + (2 * L + l) * 8, t, hh)

            def rhsHN(kc):
                return HN[:, kc, hh * T:(hh + 1) * T]

            def proj(slot, j):
                bk = rot_p.next()
                for kc in range(DC):
                    P.add("pe", lambda e, kc=kc: e.matmul(
                        ps[bk][:], lhsT=WB[slot][:, (j * 8 + kc) * 128:(j * 8 + kc + 1) * 128], rhs=rhsHN(kc),
                        start=(kc == 0), stop=(kc == DC - 1)),
                        reads=[("WB", slot), ("HN", kc, hh)], writes=[("ps", bk)])
                return bk

            wkey = ("W", "win", l)
            for i in range(2):
                s = wload(wb["win"][l, i], 2048, wkey)
                for j in range(2):
                    c = 2 * i + j
                    bk = proj(s, j)
                    P.add("act", lambda e, c=c, bk=bk: e.copy(out=KT_(c, a, b), in_=ps[bk][:]),
                          reads=[("ps", bk)], writes=[("KT", c, t)])
            for i in range(2):
                s = wload(wb["win"][l, 2 + i], 2048, wkey)
                for j in range(2):
                    c = 2 * i + j
                    bk = proj(s, j)
                    P.add("act", lambda e, c=c, bk=bk: e.mul(out=QS_(c), in_=ps[bk][:], mul=0.125),
                          reads=[("ps", bk)], writes=[("QS", c)])
                    P.add("dve", lambda e, c=c, bk=bk: e.tensor_scalar(
                        out=NQ_(c), in0=QS_(c), scalar1=-1.0, scalar2=None, op0=ALU.mult),
                        reads=[("QS", c)], writes=[("NQ", c)])
            sv = [wload(wb["wv"][l, i], 2048, ("W", "wv", l)) for i in range(2)]
            for tb in range(4):
                bk = rot_p.next()
                for kc in range(DC):
                    P.add("pe", lambda e, kc=kc, bk=bk, tb=tb: e.matmul(
                        ps[bk][:], lhsT=HN[:, kc, hh * T + tb * 128: hh * T + (tb + 1) * 128],
                        rhs=WB[sv[kc // 4]][:, (kc % 4) * 512:(kc % 4 + 1) * 512],
                        start=(kc == 0), stop=(kc == DC - 1)),
                        reads=[("WB", sv[kc // 4]), ("HN", kc, hh)], writes=[("ps", bk)])
                P.add("dve", lambda e, bk=bk, tb=tb: e.tensor_copy(out=V_(4 * t + tb), in_=ps[bk][:]),
                      reads=[("ps", bk)], writes=[("V", 4 * t + tb)])
            s = wload(wb["win"][l, 4], 2048, wkey)
            for c in range(2):
                bk = proj(s, c)
                U, B1, B2 = PM[0], PM[1], PM[2]
                P.add("act", lambda e, bk=bk: e.copy(out=PM[0][:, 16:528], in_=ps[bk][:]), reads=[("ps", bk)],
                      writes=[("PM", 0)])
                if t == 0:
                    P.add("dve", lambda e: e.memset(PM[0][:, 0:16], 0.0), writes=[("PM", 0)])
                else:
                    P.add("dve", lambda e, c=c: e.tensor_copy(out=PM[0][:, 0:16], in_=PH[:, c, :]),
                          reads=[("PH", c)], writes=[("PM", 0)])
                n0 = 1 + 2 * c
                src, dst = 0, 1
                for k in range(1, n0 + 2):
                    sh = 1 << (k - 1)
                    lo = (1 << k) - 1
                    p0 = 0 if k <= n0 else 64
                    P.add("dve", lambda e, src=src, dst=dst, sh=sh, lo=lo, p0=p0: e.tensor_tensor(
                        out=PM[dst][p0:128, lo:528], in0=PM[src][p0:128, lo:528], in1=PM[src][p0:128, lo - sh:528 - sh],
                        op=ALU.add),
                        reads=[("PM", src)], writes=[("PM", dst)])
                    if k == 1:
                        src, dst = 1, 2
                    else:
                        src, dst = dst, src
                for hlf, bufi in ((0, 1), (1, 2)):
                    p0 = 64 * hlf
                    P.add("dve", lambda e, p0=p0, bufi=bufi, c=c: e.scalar_tensor_tensor(
                        out=PD[p0:p0 + 64, :], in0=PM[bufi][p0:p0 + 64, 16:528], scalar=VEC[p0:p0 + 64, V_IW + c:V_IW + c + 1],
                        in1=PM[0][p0:p0 + 64, 16:528], op0=ALU.mult, op1=ALU.subtract),
                        reads=[("PM", bufi), ("PM", 0), "VEC"], writes=["PD"])
                if t == 0:
                    for hlf, bufi in ((0, 1), (1, 2)):
                        p0 = 64 * hlf
                        P.add("dve", lambda e, p0=p0, bufi=bufi, c=c: e.tensor_tensor(
                            out=SG[0][p0:p0 + 64, 0:16], in0=PM[bufi][p0:p0 + 64, 16:32],
                            in1=VEC[p0:p0 + 64, V_IC + c * 16:V_IC + (c + 1) * 16], op=ALU.mult),
                            reads=[("PM", bufi), "VEC"], writes=[("SG", 0)])
                    P.add("dve", lambda e: e.memset(DUM[:], 0.0), writes=["DUM"])
                    P.add("dve", lambda e: e.tensor_tensor(out=PD[:, 0:16], in0=SG[0][:, 0:16], in1=PM[0][:, 16:32],
                                                           op=ALU.subtract),
                          reads=[("SG", 0), ("PM", 0), "PD"], writes=["PD"])
                if t < NT - 1:
                    P.add("dve", lambda e, c=c: e.tensor_copy(out=PH[:, c, :], in_=PM[0][:, 512:528]),
                          reads=[("PM", 0)], writes=[("PH", c)])
                b2 = rot_p.next()
                P.add("pe", lambda e, c=c, b2=b2: e.matmul(ps[b2][:], lhsT=PW[:, (l * 2 + c) * 128:(l * 2 + c + 1) * 128],
                                                          rhs=PD[:], start=True, stop=True),
                      reads=["PW", "PD"], writes=[("ps", b2)])
                P.add("dve", lambda e, c=c, b2=b2: e.tensor_scalar(
                    out=AO[:, c, :], in0=ps[b2][:], scalar1=vcol(V_PS + l * 2 + c), scalar2=None, op0=ALU.mult),
                    reads=[("ps", b2), "VEC"], writes=[("AO", c)])
            sx = wload(wb["win"][l, 5], 2048, wkey)
            sbb = wload(wb["win"][l, 6], 2048, wkey)
            sc = wload(wb["win"][l, 7], 2048, wkey)
            for c in range(2):
                bx = proj(sx, c)
                bc = proj(sc, c)
                bb = proj(sbb, c)
                P.add("act", lambda e, bx=bx: e.copy(out=PM[1][:, 0:512], in_=ps[bx][:]), reads=[("ps", bx)],
                      writes=[("PM", 1)])
                if t == 0:
                    P.add("dve", lambda e: e.memset(PM[2][:, 0:2], 0.0), writes=[("PM", 2)])
                else:
                    P.add("dve", lambda e, c=c: e.tensor_copy(out=PM[2][:, 0:2], in_=CH[:, c, :]),
                          reads=[("CH", c)], writes=[("PM", 2)])
                P.add("dve", lambda e, bc=bc: e.tensor_tensor(out=PM[2][:, 2:514], in0=ps[bc][:], in1=PM[1][:, 0:512],
                                                              op=ALU.mult),
                      reads=[("ps", bc), ("PM", 1)], writes=[("PM", 2)])
                if t < NT - 1:
                    P.add("dve", lambda e, c=c: e.tensor_copy(out=CH[:, c, :], in_=PM[2][:, 512:514]),
                          reads=[("PM", 2)], writes=[("CH", c)])
                cw = lambda j, c=c: vcol(V_CW + (l * 3 + j) * 2 + c)
                P.add("dve", lambda e, c=c, cw=cw: e.tensor_scalar(
                    out=PM[0][:, 0:512], in0=PM[2][:, 2:514], scalar1=cw(2), scalar2=vcol(V_CB + l * 2 + c),
                    op0=ALU.mult, op1=ALU.add),
                    reads=[("PM", 2), "VEC"], writes=[("PM", 0)])
                P.add("dve", lambda e: e.memset(DUM[:], 0.0), writes=["DUM"])
                P.add("dve", lambda e, cw=cw: e.scalar_tensor_tensor(
                    out=PM[1][:, 0:512], in0=PM[2][:, 1:513], scalar=cw(1), in1=PM[0][:, 0:512], op0=ALU.mult, op1=ALU.add),
                    reads=[("PM", 2), ("PM", 0), "VEC"], writes=[("PM", 1)])
                P.add("dve", lambda e: e.memset(DUM[:], 0.0), writes=["DUM"])
                P.add("dve", lambda e, cw=cw: e.scalar_tensor_tensor(
                    out=PM[0][:, 0:512], in0=PM[2][:, 0:512], scalar=cw(0), in1=PM[1][:, 0:512], op0=ALU.mult, op1=ALU.add),
                    reads=[("PM", 2), ("PM", 1), "VEC"], writes=[("PM", 0)])
                P.add("dve", lambda e: e.memset(DUM[:], 0.0), writes=["DUM"])
                P.add("dve", lambda e, c=c, bb=bb: e.tensor_tensor(out=CO[:, c, :], in0=ps[bb][:], in1=PM[0][:, 0:512],
                                                                    op=ALU.mult),
                      reads=[("ps", bb), ("PM", 0)], writes=[("CO", c)])
            nkb = 4 * (t + 1)
            for h in range(8):
                c = h // 2
                hb = 64 * (h % 2)
                ob = rot_o.next()
                ri = rot_r.next()
                P.add("pool", lambda e, ri=ri: e.memset(R32[ri][:], 0.0), writes=[("R32", ri)])
                first = True
                for kb in reversed(range(nkb)):
                    j = kb - 4 * t
                    c0 = 128 * j if j > 0 else 0
                    zb = rot_z.next()
                    cb = rot_c.next()
                    at = rot_at.next()
                    kt_ap = KT_(c, kb * 128, (kb + 1) * 128)
                    ktkey = ("KT", c, kb // 4)
                    P.add("pe", lambda e, zb=zb, c0=c0, kt_ap=kt_ap, c=c, hb=hb: e.matmul(
                        ps[zb][:, c0:T], lhsT=kt_ap[hb:hb + 64, :], rhs=QS_(c)[hb:hb + 64, c0:T], start=True, stop=True),
                        reads=[ktkey, ("QS", c)], writes=[("ps", zb)])
                    P.add("act", lambda e, zb=zb, c0=c0, at=at: e.activation(out=AE[at][:, c0:T], in_=ps[zb][:, c0:T],
                                                                              func=AF.Exp),
                          reads=[("ps", zb)], writes=[("AE", at)])
                    P.add("act", lambda e, c0=c0, at=at: e.activation(out=ASP[at][:, c0:T], in_=AE[at][:, c0:T],
                                                                       func=AF.Ln, bias=1.0),
                          reads=[("AE", at)], writes=[("ASP", at)])
                    if j >= 0:
                        P.add("dve", lambda e, c0=c0, at=at: e.tensor_tensor(
                            out=ASP[at][:, c0:c0 + 128], in0=ASP[at][:, c0:c0 + 128], in1=MSK, op=ALU.mult),
                            reads=[("ASP", at), "CB"], writes=[("ASP", at)])
                    P.add("pe", lambda e, cb=cb, c0=c0, at=at: e.matmul(
                        ps[cb][:, c0:T], lhsT=TRI, rhs=ASP[at][:, c0:T], start=True, stop=False),
                        reads=[("ASP", at), "CB"], writes=[("ps", cb)])
                    P.add("pe", lambda e, cb=cb, c0=c0, kt_ap=kt_ap, c=c, hb=hb, first=first: e.matmul(
                        ps[cb][:, c0:T], lhsT=kt_ap[hb:hb + 64, :], rhs=NQ_(c)[hb:hb + 64, c0:T], start=False,
                        stop=first),
                        reads=[ktkey, ("NQ", c)], writes=[("ps", cb)])
                    if not first:
                        P.add("pe", lambda e, cb=cb, c0=c0, ri=ri: e.matmul(
                            ps[cb][:, c0:T], lhsT=ONES1[:], rhs=RB[ri][:, c0:T], start=False, stop=True),
                            reads=[("RB", ri), "ONES1"], writes=[("ps", cb)])
                    P.add("act", lambda e, cb=cb, c0=c0, at=at: e.activation(out=AW[at][:, c0:T], in_=ps[cb][:, c0:T],
                                                                              func=AF.Exp, scale=-1.0),
                          reads=[("ps", cb)], writes=[("AW", at)])
                    if j >= 0:
                        P.add("dve", lambda e, c0=c0, at=at: e.tensor_tensor(
                            out=AW[at][:, c0:c0 + 128], in0=AW[at][:, c0:c0 + 128], in1=MSK, op=ALU.mult),
                            reads=[("AW", at), "CB"], writes=[("AW", at)])
                    P.add("pe", lambda e, ob=ob, c0=c0, at=at, kb=kb, c=c, first=first: e.matmul(
                        ps[ob][:, c0:T], lhsT=V_(kb)[:, c * 128:(c + 1) * 128], rhs=AW[at][:, c0:T], start=first,
                        stop=(kb == 0), skip_group_check=True),
                        reads=[("V", kb), ("AW", at)], writes=[("ps", ob)])
                    if kb > 0:
                        jn = kb - 1 - 4 * t
                        c0n = 128 * jn if jn > 0 else 0
                        P.add("pool", lambda e, c0=c0, at=at, ri=ri: e.tensor_tensor(
                            out=R32[ri][:, c0:T], in0=R32[ri][:, c0:T], in1=ASP[at][:, c0:T], op=ALU.add),
                            reads=[("R32", ri), ("ASP", at)], writes=[("R32", ri)])
                        P.add("pool", lambda e, c0n=c0n, ri=ri: e.tensor_copy(out=RB[ri][:, c0n:T], in_=R32[ri][:, c0n:T]),
                              reads=[("R32", ri)], writes=[("RB", ri)])
                    first = False
                P.add("act", lambda e, ob=ob, c=c, hb=hb: e.copy(out=BO[hb:hb + 64, c, :], in_=ps[ob][hb:hb + 64, :]),
                      reads=[("ps", ob)], writes=[("BO", c)])
            for n in range(DC):
                sA = wload(wb["wgb"][l, n, 0], 2048, ("W", "wgb", l))
                sB = wload(wb["wgb"][l, n, 1], 2048, ("W", "wgb", l))
                gb = [rot_p.next() for _ in range(3)]
                bb_ = [rot_p.next() for _ in range(3)]
                for gi in range(3):
                    sl, off = (sA, gi * 8) if gi < 2 else (sB, 0)
                    for kc in range(DC):
                        P.add("pe", lambda e, kc=kc, sl=sl, off=off, bk=gb[gi]: e.matmul(
                            ps[bk][:], lhsT=WB[sl][:, (off + kc) * 128:(off + kc + 1) * 128], rhs=rhsHN(kc),
                            start=(kc == 0), stop=(kc == DC - 1)),
                            reads=[("WB", sl), ("HN", kc, hh)], writes=[("ps", gb[gi])])
                for bi, (off, nk, src, key) in enumerate(((8, 2, AO, "AO"), (10, 4, BO, "BO"), (14, 2, CO, "CO"))):
                    for kc in range(nk):
                        P.add("pe", lambda e, kc=kc, off=off, nk=nk, src=src, bk=bb_[bi], sB=sB: e.matmul(
                            ps[bk][:], lhsT=WB[sB][:, (off + kc) * 128:(off + kc + 1) * 128], rhs=src[:, kc, :],
                            start=(kc == 0), stop=(kc == nk - 1)),
                            reads=[("WB", sB), (key, kc)], writes=[("ps", bb_[bi])])
                ks = [rot_sg.next() for _ in range(3)]
                for gi in range(3):
                    P.add("act", lambda e, gi=gi, k=ks[gi], bk=gb[gi], n=n: e.activation(
                        out=SG[k][:], in_=ps[bk][:], func=AF.Sigmoid, bias=vcol(V_BG + (l * 3 + gi) * 8 + n)),
                        reads=[("ps", gb[gi]), "VEC"], writes=[("SG", ks[gi])])
                for gi in range(3):
                    P.add("dve", lambda e, k=ks[gi], bk=bb_[gi]: e.tensor_tensor(
                        out=SG[k][:], in0=ps[bk][:], in1=SG[k][:], op=ALU.mult),
                        reads=[("ps", bb_[gi]), ("SG", ks[gi])], writes=[("SG", ks[gi])])
                P.add("pool", lambda e, k0=ks[0], k1=ks[1]: e.tensor_tensor(out=SG[k0][:], in0=SG[k0][:], in1=SG[k1][:],
                                                                             op=ALU.add),
                      reads=[("SG", ks[0]), ("SG", ks[1])], writes=[("SG", ks[0])])
                P.add("pool", lambda e, k0=ks[0], k2=ks[2], n=n: e.tensor_tensor(
                    out=HN[:, n, mh * T:(mh + 1) * T], in0=SG[k0][:], in1=SG[k2][:], op=ALU.add),
                    reads=[("SG", ks[0]), ("SG", ks[2])], writes=[("HN", n, mh)])
            pend = None
            for i in range(4):
                s = wload(wb["wo"][l, i], 2048, ("W", "wo", l))
                for j in range(2):
                    n = 2 * i + j
                    bk = rot_p.next()
                    for kc in range(DC):
                        P.add("pe", lambda e, kc=kc, s=s, j=j, bk=bk: e.matmul(
                            ps[bk][:], lhsT=WB[s][:, (j * 8 + kc) * 128:(j * 8 + kc + 1) * 128],
                            rhs=HN[:, kc, mh * T:(mh + 1) * T], start=(kc == 0), stop=(kc == DC - 1)),
                            reads=[("WB", s), ("HN", kc, mh)], writes=[("ps", bk)])
                    if pend is not None:
                        pend()
                    pend = evac_h_and_stats(bk, n, None)
            pend()
            postnorm_update(V_G + (3 * L + l) * 8, t, 1.0)

        for s_ in range(NS):
            for c in range(DC):
                P.add("sp", lambda e, c=c, s_=s_: e.dma_start(out=X[:, c, :], in_=xT[s_, c]),
                      reads=(WKEYS if s_ == 0 else []), writes=[("X", c, t) for t in range(NT)], dma_key="X%d" % c)
            for l in range(NL):
                if "ffn1" in parts:
                    ffn(l, 0, 0)
                    ffn(l, 0, 1)
                fence()
                if "mix" in parts:
                    for t in range(NT):
                        mix(l, t)
                fence()
                if "ffn2" in parts:
                    ffn(l, 1, 0)
                    ffn(l, 1, 1)
            for c in range(DC):
                P.add("sp", lambda e, c=c, s_=s_: e.dma_start(out=yT[s_, c], in_=X[:, c, :]),
                      reads=[("X", c, t) for t in range(NT)], writes=[("Y", s_, c)], dma_key="X%d" % c)
        P.add("sp", lambda e: None, reads=[("Y", s_, c) for s_ in range(NS) for c in range(DC)])
        P.emit(nc, st)
    return nc, len(P.ops)


def _kcp(w):
    K, M = w.shape
    return w.reshape(K // 128, 128, M // 128, 128).transpose(2, 1, 0, 3)


def pack_weights(inp):
    f = np.float32
    W = {}
    wgu = np.empty((L, 2, FC, 128, 16, 128), f)
    wd = np.empty((L, 2, DC, 2, 128, 11, 128), f)
    win = np.empty((L, 8, 128, 2, 8, 128), f)
    wv = np.empty((L, 2, 128, 4, 512), f)
    wgb = np.empty((L, DC, 2, 128, 16, 128), f)
    wo = np.empty((L, 4, 128, 2, 8, 128), f)
    cols = np.concatenate([np.arange(768, 1280), np.arange(256, 768), np.arange(0, 256), np.arange(1792, 2560)])
    for l in range(L):
        for fi, pre in enumerate(("ffn1", "ffn2")):
            wgu[l, fi, :, :, 0:8] = _kcp(np.asarray(inp[pre + "_w_gate"][l]))
            wgu[l, fi, :, :, 8:16] = _kcp(np.asarray(inp[pre + "_w_up"][l]))
            d = _kcp(np.asarray(inp[pre + "_w_down"][l]))
            wd[l, fi] = d.reshape(DC, 128, 2, 11, 128).transpose(0, 2, 1, 3, 4)
        wi = np.asarray(inp["w_in"][l])
        p = _kcp(wi[:, cols])
        win[l] = p.reshape(8, 2, 128, 8, 128).transpose(0, 2, 1, 3, 4)
        v = wi[:, 1280:1792].reshape(2, 4, 128, 512)
        wv[l] = v.transpose(0, 2, 1, 3)
        g = [_kcp(wi[:, 2560 + i * 1024: 2560 + (i + 1) * 1024]) for i in range(3)]
        wgb[l, :, 0, :, 0:8] = g[0]
        wgb[l, :, 0, :, 8:16] = g[1]
        wgb[l, :, 1, :, 0:8] = g[2]
        wgb[l, :, 1, :, 8:10] = _kcp(np.asarray(inp["w_br_pool"][l]))
        wgb[l, :, 1, :, 10:14] = _kcp(np.asarray(inp["w_br_sb"][l]))
        wgb[l, :, 1, :, 14:16] = _kcp(np.asarray(inp["w_br_conv"][l]))
        o = _kcp(np.asarray(inp["w_out"][l]))
        wo[l] = o.reshape(4, 2, 128, 8, 128).transpose(0, 2, 1, 3, 4)
    W["wgu"] = wgu.reshape(L, 2, FC, 128, 2048)
    W["wd"] = wd.reshape(L, 2, DC, 2, 128, 1408)
    W["win"] = win.reshape(L, 8, 128, 2048)
    W["wv"] = wv.reshape(L, 2, 128, 2048)
    W["wgb"] = wgb.reshape(L, DC, 2, 128, 2048)
    W["wo"] = wo.reshape(L, 4, 128, 2048)
    vec = np.zeros((128, NV), f)
    gn = ("ffn1_pre_g", "ffn1_post_g", "mix_pre_g", "mix_post_g", "ffn2_pre_g", "ffn2_post_g")
    for gi, name in enumerate(gn):
        g_ = np.asarray(inp[name])
        vec[:, V_G + gi * L * 8: V_G + (gi + 1) * L * 8] = g_.reshape(L, 8, 128).transpose(2, 0, 1).reshape(128, L * 8)
    bg = np.asarray(inp["b_gate"])
    vec[:, V_BG:V_BG + L * 3 * 8] = bg.reshape(L, 3, 8, 128).transpose(3, 0, 1, 2).reshape(128, L * 24)
    vec[:, V_PS:V_PS + L * 2] = np.asarray(inp["pool_scale"]).reshape(L, 2, 128).transpose(2, 0, 1).reshape(128, L * 2)
    vec[:, V_CW:V_CW + L * 6] = np.asarray(inp["conv_w"]).reshape(L, 3, 2, 128).transpose(3, 0, 1, 2).reshape(128, L * 6)
    vec[:, V_CB:V_CB + L * 2] = np.asarray(inp["conv_b"]).reshape(L, 2, 128).transpose(2, 0, 1).reshape(128, L * 2)
    wins = (2, 4, 8, 16)
    for c in range(2):
        for hlf in range(2):
            w_ = wins[2 * c + hlf]
            vec[64 * hlf:64 * hlf + 64, V_IW + c] = 1.0 / w_
            for t in range(16):
                vec[64 * hlf:64 * hlf + 64, V_IC + c * 16 + t] = 1.0 / min(t + 1, w_)
    W["vec"] = vec
    cst = np.zeros((128, 256), f)
    i = np.arange(128)
    cst[:, 0:128] = (i[:, None] >= i[None, :]).astype(f)
    cst[:, 128:256] = (i[None, :] > i[:, None]).astype(f)
    W["cst"] = cst
    pw = np.zeros((128, L, 2, 128), f)
    pool_w = np.asarray(inp["pool_w"])
    for l in range(L):
        for c in range(2):
            for hlf in range(2):
                pw[64 * hlf:64 * hlf + 64, l, c, 64 * hlf:64 * hlf + 64] = pool_w[l, 2 * c + hlf]
    W["poolw"] = pw.reshape(128, L * 2 * 128)
    return W


_NC_CACHE = {}


def kernel(**inputs):
    x = np.asarray(inputs["x"], dtype=np.float32)
    B = x.shape[0]
    W = pack_weights(inputs)
    key = (SEQ_PER_CORE, L)
    if key not in _NC_CACHE:
        _NC_CACHE[key] = build_program(SEQ_PER_CORE, L)[0]
    nc = _NC_CACHE[key]
    in_maps = []
    for c in range(NCORE):
        xs = x[c * SEQ_PER_CORE:(c + 1) * SEQ_PER_CORE]
        xT = np.ascontiguousarray(xs.transpose(0, 2, 1)).reshape(SEQ_PER_CORE, DC, 128, S)
        m = {"xT": xT}
        m.update(W)
        in_maps.append(m)
    res = run_bass_kernel_spmd(nc, in_maps, core_ids=list(range(NCORE)))
    out = np.empty((B, S, D), np.float32)
    for c in range(NCORE):
        yT = np.asarray(res.results[c]["yT"]).reshape(SEQ_PER_CORE, D, S)
        out[c * SEQ_PER_CORE:(c + 1) * SEQ_PER_CORE] = yT.transpose(0, 2, 1)
    return out
```

```python
import numpy as np
from contextlib import ExitStack
import concourse.bass as bass
import concourse.mybir as mybir
from concourse.bass_utils import run_bass_kernel_spmd

F32 = mybir.dt.float32
BF16 = mybir.dt.bfloat16
AF = mybir.ActivationFunctionType
ALU = mybir.AluOpType

L = 4
D = 1024
S = 2048
T = 512
NT = 4
DC = 8
FC = 22
NCORE = 8
SEQ_PER_CORE = 4
EPS = 1e-6
NV = 362
V_G, V_BG, V_PS, V_CW, V_CB, V_IW, V_IC = 0, 192, 288, 296, 320, 328, 330

EPOCH = 30000
import os
NDUMMY = int(os.environ.get('NDUMMY', '0'))


class Op:
    __slots__ = ("eng", "fn", "reads", "writes", "dma_key", "seq", "waits", "signal", "rank", "clock", "res")

    def __init__(self, eng, fn, reads, writes, dma_key):
        self.eng = eng
        self.fn = fn
        self.reads = reads
        self.writes = writes
        self.dma_key = dma_key
        self.seq = 0
        self.waits = []
        self.signal = False
        self.rank = 0
        self.clock = None
        self.res = None


class Prog:
    ENGS = ("pe", "act", "dve", "pool", "sp")

    def __init__(self):
        self.ops = []

    def add(self, eng, fn, reads=(), writes=(), dma_key=None):
        reads = tuple(reads)
        writes = tuple(writes) + tuple(k for k in reads if isinstance(k, tuple) and k[0] == "ps")
        self.ops.append(Op(eng, fn, reads, writes, dma_key))

    def analyze(self):
        last_writer = {}
        readers = {}
        seqctr = {}
        known = {e: {} for e in self.ENGS}
        by_res = {}
        for op in self.ops:
            res = ("dma:" + str(op.dma_key)) if op.dma_key is not None else op.eng
            seqctr[res] = seqctr.get(res, 0) + 1
            op.seq = seqctr[res]
            op.res = res
            by_res[(res, op.seq)] = op
            deps = set()
            for r in op.reads:
                w = last_writer.get(r)
                if w is not None:
                    deps.add(w)
            for w_ in op.writes:
                w = last_writer.get(w_)
                if w is not None:
                    deps.add(w)
                for rd in readers.get(w_, ()):
                    deps.add(rd)
            deps.discard(op)
            kn = known[op.eng]
            need = {}
            for d in deps:
                if d.res == op.eng and op.dma_key is None and op.eng == "pe":
                    continue
                if kn.get(d.res, 0) >= d.seq:
                    continue
                if need.get(d.res, 0) < d.seq:
                    need[d.res] = d.seq
            items = sorted(need.items())
            final = []
            for r, v in items:
                implied = False
                for r2, v2 in items:
                    if (r2, v2) == (r, v):
                        continue
                    if by_res[(r2, v2)].clock.get(r, 0) >= v:
                        implied = True
                        break
                if not implied:
                    final.append((r, v))
            op.waits = final
            for r, v in final:
                d = by_res[(r, v)]
                d.signal = True
                for rr, vv in d.clock.items():
                    if kn.get(rr, 0) < vv:
                        kn[rr] = vv
            ck = dict(kn)
            ck[res] = op.seq
            op.clock = ck
            for r in op.reads:
                readers.setdefault(r, []).append(op)
            for w_ in op.writes:
                last_writer[w_] = op
                readers[w_] = []
        rk = {}
        for op in self.ops:
            if op.dma_key is None:
                if op.signal:
                    rk[op.res] = rk.get(op.res, 0) + 1
                    op.rank = rk[op.res]
            else:
                op.rank = op.seq
        self.nsig = rk
        self.by_res = by_res

    def emit(self, nc, stack):
        self.analyze()
        sems = {}
        for e in self.ENGS:
            n = self.nsig.get(e, 0)
            for k in range((n + EPOCH - 1) // EPOCH):
                sems[(e, k)] = stack.enter_context(nc.semaphore("s_%s_%d" % (e, k)))
        dkeys = sorted({op.res for op in self.ops if op.dma_key is not None})
        for i, r in enumerate(dkeys):
            sems[(r, 0)] = stack.enter_context(nc.semaphore("sd_%d" % i))
        by_res = self.by_res

        def sem_val(r, v):
            d = by_res[(r, v)]
            if d.dma_key is not None:
                return sems[(r, 0)], 16 * d.rank
            k = (d.rank - 1) // EPOCH
            return sems[(r, k)], (d.rank - 1) % EPOCH + 1

        per_eng = {e: [] for e in self.ENGS}
        for op in self.ops:
            per_eng[op.eng].append(op)
        block = stack.enter_context(nc.Block())

        def run(e, ops):
            def body(eng):
                for op in ops:
                    for r, v in op.waits:
                        s, val = sem_val(r, v)
                        eng.wait_ge(s, val)
                    ins = op.fn(eng)
                    if ins is None:
                        continue
                    if op.dma_key is not None:
                        ins.then_inc(sems[(op.res, 0)], 16)
                    elif op.signal:
                        k = (op.rank - 1) // EPOCH
                        ins.then_inc(sems[(e, k)], 1)
            return body

        block.tensor(run("pe", per_eng["pe"]))
        block.scalar(run("act", per_eng["act"]))
        block.vector(run("dve", per_eng["dve"]))
        block.gpsimd(run("pool", per_eng["pool"]))
        block.sync(run("sp", per_eng["sp"]))


class Rot:
    def __init__(self, items):
        self.items = list(items)
        self.i = 0

    def next(self):
        v = self.items[self.i % len(self.items)]
        self.i += 1
        return v


def build_program(NS=SEQ_PER_CORE, NL=L, parts=("ffn1", "mix", "ffn2")):
    nc = bass.Bass("TRN2", target_bir_lowering=False)

    def din(name, shape, dt=F32):
        return nc.dram_tensor(name, list(shape), dt, kind="ExternalInput").ap()

    def dscr(name, shape):
        return nc.dram_tensor(name, list(shape), BF16, kind="Internal").ap()

    xT = din("xT", [NS, DC, 128, S])
    yT = nc.dram_tensor("yT", [NS, DC, 128, S], F32, kind="ExternalOutput").ap()
    wshapes = {
        "wgu": [L, 2, FC, 128, 16 * 128],
        "wd": [L, 2, DC, 2, 128, 11 * 128],
        "win": [L, 8, 128, 16 * 128],
        "wv": [L, 2, 128, 4 * 512],
        "wgb": [L, DC, 2, 128, 16 * 128],
        "wo": [L, 4, 128, 16 * 128],
    }
    wf = {k: din(k, v) for k, v in wshapes.items()}
    wb = {k: dscr(k + "_b", v) for k, v in wshapes.items()}
    vec_d = din("vec", [128, NV])
    cst_d = din("cst", [128, 384])
    pw_d = din("poolw", [128, L * 2 * 128])

    P = Prog()
    with ExitStack() as st:
        def sb(name, shape, dt):
            return st.enter_context(nc.sbuf_tensor(name, list(shape), dt))

        X = sb("X", [128, DC, S], F32)
        HN = sb("HN", [128, DC, 2 * T], BF16)
        RA = sb("RA", [128, FC * 2 * T], BF16)
        H = sb("H", [128, DC, T], F32)
        PM = [sb("PM%d" % i, [128, 528], F32) for i in range(3)]
        PH = sb("PH", [128, 2, 16], F32)
        CH = sb("CH", [128, 2, 2], F32)
        PD = sb("PD", [128, T], BF16)
        AO = sb("AO", [128, 2, T], BF16)
        BO = sb("BO", [128, 4, T], BF16)
        CO = sb("CO", [128, 2, T], BF16)
        SG = [sb("SG%d" % i, [128, T], F32) for i in range(4)]
        ASP = [sb("ASP%d" % i, [128, T], BF16) for i in range(4)]
        AW = [sb("AW%d" % i, [128, T], BF16) for i in range(4)]
        R32 = [sb("R32_%d" % i, [128, T], F32) for i in range(2)]
        RB = [sb("RB%d" % i, [128, T], BF16) for i in range(4)]
        WB = [sb("WB%d" % i, [128, 2048], BF16) for i in range(4)]
        CB = sb("CB", [128, 384], BF16)
        PW = sb("PW", [128, L * 2 * 128], BF16)
        ONESM = sb("ONESM", [128, 128], BF16)
        ONES1 = sb("ONES1", [128, 128], BF16)
        VEC = sb("VEC", [128, NV], F32)
        RSTD = [sb("RSTD%d" % i, [128, T], F32) for i in range(2)]
        SQ = [sb("SQ%d" % i, [128, T], BF16) for i in range(2)]
        DUM = sb("DUM", [128, 2], F32)
        ps = [st.enter_context(nc.psum_tensor("ps%d" % i, [128, T], F32)) for i in range(8)]

        TRI = CB[:, 0:128]
        MSK = CB[:, 128:256]
        NTRI = CB[:, 256:384]

        def A_(m, tt):
            return RA[:, m * 1024 + tt * 512: m * 1024 + (tt + 1) * 512]

        def KT_(c, a, b):
            return RA[:, c * S + a: c * S + b]

        def V_(kb):
            return RA[:, 8192 + kb * 512: 8192 + (kb + 1) * 512]

        def QS_(c):
            return RA[:, 16384 + c * 512: 16384 + (c + 1) * 512]

        def NQ_(c):
            return RA[:, 18432 + c * 512: 18432 + (c + 1) * 512]

        A_KEYS = [("A", m, tt) for m in range(FC) for tt in range(2)]
        M_KEYS = ([("KT", c, t) for c in range(4) for t in range(NT)] + [("V", kb) for kb in range(16)]
                  + [("QS", c) for c in range(4)] + [("NQ", c) for c in range(4)])

        def vcol(i):
            return VEC[:, i:i + 1]

        rot_rstd = Rot([0, 1])
        rot_sq = Rot([0, 1])
        rot_wb = Rot([0, 1, 2, 3])
        rot_sg = Rot([0, 1, 2, 3])

        P.add("sp", lambda e: e.dma_start(out=VEC[:], in_=vec_d), writes=["VEC"], dma_key="VEC")
        Hflat = H[:, :, :]
        P.add("sp", lambda e: e.dma_start(out=H[:, 0, 0:384], in_=cst_d), writes=[("H", 0)], dma_key="H0")
        P.add("sp", lambda e: e.dma_start(out=H[:, 2, :], in_=pw_d[:, 0:512]), writes=[("H", 2)], dma_key="H2")
        P.add("sp", lambda e: e.dma_start(out=H[:, 3, :], in_=pw_d[:, 512:1024]), writes=[("H", 3)], dma_key="H3")
        P.add("dve", lambda e: e.tensor_copy(out=CB[:], in_=H[:, 0, 0:384]), reads=[("H", 0)], writes=["CB"])
        P.add("dve", lambda e: e.tensor_copy(out=PW[:, 0:512], in_=H[:, 2, :]), reads=[("H", 2)], writes=["PW"])
        P.add("dve", lambda e: e.tensor_copy(out=PW[:, 512:1024], in_=H[:, 3, :]), reads=[("H", 3)], writes=["PW"])
        P.add("dve", lambda e: e.memset(ONESM[:], 1.0 / D), writes=["ONESM"])
        P.add("dve", lambda e: e.memset(ONES1[:], -1.0), writes=["ONES1"])
        cvi = [0]

        WKEYS = []

        def convert(name, idx):
            WKEYS.append(("W", name) + tuple(idx))
            src = wf[name]
            dst = wb[name]
            for i in idx:
                src = src[i]
                dst = dst[i]
            k = cvi[0]
            cvi[0] += 1
            P.add("pool", lambda e, s_=src, d_=dst: e.dma_start(out=d_, in_=s_),
                  writes=[("W", name) + tuple(idx)], dma_key="cv%d" % k)

        for l in range(NL):
            convert("wgu", (l, 0))
            convert("wd", (l, 0))
            convert("win", (l,))
            convert("wv", (l,))
            convert("wgb", (l,))
            convert("wo", (l,))
            convert("wgu", (l, 1))
            convert("wd", (l, 1))

        def wload(src, ncols, rkey):
            s = rot_wb.next()
            P.add("sp", lambda e, s=s, src=src: e.dma_start(out=WB[s][:, 0:ncols], in_=src),
                  reads=[rkey], writes=[("WB", s)], dma_key="WB%d" % s)
            return s

        def rstd_finish(r):
            P.add("act", lambda e: e.activation(out=RSTD[r][:], in_=ps[6][:], func=AF.Ln, bias=EPS),
                  reads=[("ps", 6)], writes=[("RSTD", r)])
            P.add("act", lambda e: e.activation(out=RSTD[r][:], in_=RSTD[r][:], func=AF.Exp, scale=-0.5),
                  reads=[("RSTD", r)], writes=[("RSTD", r)])

        def prenorm(gcol, tile, hh):
            a, b = tile * T, (tile + 1) * T
            r = rot_rstd.next()
            for c in range(DC):
                q = rot_sq.next()
                P.add("act", lambda e, c=c, q=q: e.activation(out=SQ[q][:], in_=X[:, c, a:b], func=AF.Square),
                      reads=[("X", c, tile)], writes=[("SQ", q)])
                P.add("pe", lambda e, c=c, q=q: e.matmul(ps[6][:], lhsT=ONESM[:], rhs=SQ[q][:], start=(c == 0),
                                                         stop=(c == DC - 1)),
                      reads=[("SQ", q), "ONESM"], writes=[("ps", 6)])
            rstd_finish(r)
            for c in range(DC):
                P.add("dve", lambda e, c=c: e.scalar_tensor_tensor(
                    out=HN[:, c, hh * T:(hh + 1) * T], in0=X[:, c, a:b], scalar=vcol(gcol + c), in1=RSTD[r][:],
                    op0=ALU.mult, op1=ALU.mult),
                    reads=[("X", c, tile), ("RSTD", r), "VEC"], writes=[("HN", c, hh)])

        def postnorm_update(gcol, tile, factor):
            a, b = tile * T, (tile + 1) * T
            r = rot_rstd.next()
            rstd_finish(r)
            pend = None
            for c in range(DC):
                k = rot_sg.next()
                P.add("dve", lambda e, c=c, k=k: e.scalar_tensor_tensor(
                    out=SG[k][:], in0=H[:, c, :], scalar=vcol(gcol + c), in1=RSTD[r][:], op0=ALU.mult, op1=ALU.mult),
                    reads=[("H", c), ("RSTD", r), "VEC"], writes=[("SG", k)])

                def upd(c=c, k=k):
                    P.add("dve", lambda e: e.scalar_tensor_tensor(
                        out=X[:, c, a:b], in0=SG[k][:], scalar=float(factor), in1=X[:, c, a:b], op0=ALU.mult,
                        op1=ALU.add),
                        reads=[("SG", k), ("X", c, tile)], writes=[("X", c, tile)])
                if pend is not None:
                    pend()
                pend = upd
            pend()

        def evac_h_and_stats(bank, n, pending):
            q = rot_sq.next()
            P.add("dve", lambda e: e.tensor_copy(out=H[:, n, :], in_=ps[bank][:]), reads=[("ps", bank)],
                  writes=[("H", n)])
            P.add("act", lambda e: e.activation(out=SQ[q][:], in_=H[:, n, :], func=AF.Square),
                  reads=[("H", n)], writes=[("SQ", q)])

            def stat():
                P.add("pe", lambda e: e.matmul(ps[6][:], lhsT=ONESM[:], rhs=SQ[q][:], start=(n == 0),
                                               stop=(n == DC - 1)),
                      reads=[("SQ", q), "ONESM"], writes=[("ps", 6)])
            return stat

        rot_g = Rot([0, 1])
        rot_u = Rot([2, 3])
        rot_d = Rot([4, 5])

        def ffn(l, f, hf):
            gpre = V_G + ((0 if f == 0 else 4) * L + l) * 8
            gpost = V_G + ((1 if f == 0 else 5) * L + l) * 8
            for tt in range(2):
                prenorm(gpre, 2 * hf + tt, tt)
            for m in range((FC if "mlim" not in parts else 2) if "nogu" not in parts else 0):
                s = wload(wb["wgu"][l, f, m], 2048, ("W", "wgu", l, f))
                for tt in range(2):
                    g = rot_g.next()
                    u = rot_u.next()
                    for kc in range(DC):
                        P.add("pe", lambda e, kc=kc, s=s, g=g, tt=tt: e.matmul(
                            ps[g][:], lhsT=WB[s][:, kc * 128:(kc + 1) * 128], rhs=HN[:, kc, tt * T:(tt + 1) * T],
                            start=(kc == 0), stop=(kc == DC - 1)),
                            reads=[("WB", s), ("HN", kc, tt)], writes=[("ps", g)])
                    for kc in range(DC):
                        P.add("pe", lambda e, kc=kc, s=s, u=u, tt=tt: e.matmul(
                            ps[u][:], lhsT=WB[s][:, (8 + kc) * 128:(9 + kc) * 128], rhs=HN[:, kc, tt * T:(tt + 1) * T],
                            start=(kc == 0), stop=(kc == DC - 1)),
                            reads=[("WB", s), ("HN", kc, tt)], writes=[("ps", u)])
                    k = rot_sg.next()
                    P.add("act", lambda e, g=g, k=k: e.activation(out=SG[k][:], in_=ps[g][:], func=AF.Silu),
                          reads=[("ps", g)], writes=[("SG", k)])
                    P.add("dve", lambda e, u=u, k=k, m=m, tt=tt: e.tensor_tensor(
                        out=A_(m, tt), in0=ps[u][:], in1=SG[k][:], op=ALU.mult),
                        reads=[("ps", u), ("SG", k)], writes=[("A", m, tt)])
            for tt in range(2 if "nodown" not in parts else 0):
                pend = None
                for n in range(DC):
                    s0 = wload(wb["wd"][l, f, n, 0], 1408, ("W", "wd", l, f))
                    s1 = wload(wb["wd"][l, f, n, 1], 1408, ("W", "wd", l, f))
                    d = rot_d.next()
                    for kc in range(FC):
                        sl = s0 if kc < 11 else s1
                        P.add("pe", lambda e, kc=kc, sl=sl, d=d, tt=tt: e.matmul(
                            ps[d][:], lhsT=WB[sl][:, (kc % 11) * 128:(kc % 11 + 1) * 128], rhs=A_(kc, tt),
                            start=(kc == 0), stop=(kc == FC - 1)),
                            reads=[("WB", sl), ("A", kc, tt)], writes=[("ps", d)])
                    if pend is not None:
                        pend()
                    pend = evac_h_and_stats(d, n, None)
                pend()
                postnorm_update(gpost, 2 * hf + tt, 0.5)

        def fence():
            P.add("dve", lambda e: e.memset(DUM[:], 0.0), writes=A_KEYS + M_KEYS + ["DUM"])

        rot_p = Rot([0, 1, 2, 3, 4, 5])
        rot_z = Rot([0, 1])
        rot_c = Rot([2, 3])
        rot_o = Rot([4, 5])
        rot_asp = Rot([0, 1, 2, 3])
        rot_aw = Rot([0, 1, 2, 3])
        rot_r = Rot([0, 1])

        def mix(l, t):
            hh = t % 2
            mh = 1 - hh
            a, b = t * T, (t + 1) * T
            prenorm(V_G + (2 * L + l) * 8, t, hh)

            def rhsHN(kc):
                return HN[:, kc, hh * T:(hh + 1) * T]

            def proj(slot, j):
                bk = rot_p.next()
                for kc in range(DC):
                    P.add("pe", lambda e, kc=kc: e.matmul(
                        ps[bk][:], lhsT=WB[slot][:, (j * 8 + kc) * 128:(j * 8 + kc + 1) * 128], rhs=rhsHN(kc),
                        start=(kc == 0), stop=(kc == DC - 1)),
                        reads=[("WB", slot), ("HN", kc, hh)], writes=[("ps", bk)])
                return bk

            wkey = ("W", "win", l)
            for i in range(2):
                s = wload(wb["win"][l, i], 2048, wkey)
                for j in range(2):
                    c = 2 * i + j
                    bk = proj(s, j)
                    P.add("act", lambda e, c=c, bk=bk: e.copy(out=KT_(c, a, b), in_=ps[bk][:]),
                          reads=[("ps", bk)], writes=[("KT", c, t)])
            for i in range(2):
                s = wload(wb["win"][l, 2 + i], 2048, wkey)
                for j in range(2):
                    c = 2 * i + j
                    bk = proj(s, j)
                    P.add("act", lambda e, c=c, bk=bk: e.mul(out=QS_(c), in_=ps[bk][:], mul=0.125),
                          reads=[("ps", bk)], writes=[("QS", c)])
            sv = [wload(wb["wv"][l, i], 2048, ("W", "wv", l)) for i in range(2)]
            for tb in range(4):
                bk = rot_p.next()
                for kc in range(DC):
                    P.add("pe", lambda e, kc=kc, bk=bk, tb=tb: e.matmul(
                        ps[bk][:], lhsT=HN[:, kc, hh * T + tb * 128: hh * T + (tb + 1) * 128],
                        rhs=WB[sv[kc // 4]][:, (kc % 4) * 512:(kc % 4 + 1) * 512],
                        start=(kc == 0), stop=(kc == DC - 1)),
                        reads=[("WB", sv[kc // 4]), ("HN", kc, hh)], writes=[("ps", bk)])
                P.add("dve", lambda e, bk=bk, tb=tb: e.tensor_copy(out=V_(4 * t + tb), in_=ps[bk][:]),
                      reads=[("ps", bk)], writes=[("V", 4 * t + tb)])
            s = wload(wb["win"][l, 4], 2048, wkey)
            for c in range(2):
                bk = proj(s, c)
                U, B1, B2 = PM[0], PM[1], PM[2]
                P.add("act", lambda e, bk=bk: e.copy(out=PM[0][:, 16:528], in_=ps[bk][:]), reads=[("ps", bk)],
                      writes=[("PM", 0)])
                if t == 0:
                    P.add("dve", lambda e: e.memset(PM[0][:, 0:16], 0.0), writes=[("PM", 0)])
                else:
                    P.add("dve", lambda e, c=c: e.tensor_copy(out=PM[0][:, 0:16], in_=PH[:, c, :]),
                          reads=[("PH", c)], writes=[("PM", 0)])
                n0 = 1 + 2 * c
                src, dst = 0, 1
                for k in range(1, n0 + 2):
                    sh = 1 << (k - 1)
                    lo = (1 << k) - 1
                    p0 = 0 if k <= n0 else 64
                    P.add("dve", lambda e, src=src, dst=dst, sh=sh, lo=lo, p0=p0: e.tensor_tensor(
                        out=PM[dst][p0:128, lo:528], in0=PM[src][p0:128, lo:528], in1=PM[src][p0:128, lo - sh:528 - sh],
                        op=ALU.add),
                        reads=[("PM", src)], writes=[("PM", dst)])
                    if k == 1:
                        src, dst = 1, 2
                    else:
                        src, dst = dst, src
                for hlf, bufi in ((0, 1), (1, 2)):
                    p0 = 64 * hlf
                    P.add("dve", lambda e, p0=p0, bufi=bufi, c=c: e.scalar_tensor_tensor(
                        out=PD[p0:p0 + 64, :], in0=PM[bufi][p0:p0 + 64, 16:528], scalar=VEC[p0:p0 + 64, V_IW + c:V_IW + c + 1],
                        in1=PM[0][p0:p0 + 64, 16:528], op0=ALU.mult, op1=ALU.subtract),
                        reads=[("PM", bufi), ("PM", 0), "VEC"], writes=["PD"])
                if t == 0:
                    for hlf, bufi in ((0, 1), (1, 2)):
                        p0 = 64 * hlf
                        P.add("dve", lambda e, p0=p0, bufi=bufi, c=c: e.tensor_tensor(
                            out=SG[0][p0:p0 + 64, 0:16], in0=PM[bufi][p0:p0 + 64, 16:32],
                            in1=VEC[p0:p0 + 64, V_IC + c * 16:V_IC + (c + 1) * 16], op=ALU.mult),
                            reads=[("PM", bufi), "VEC"], writes=[("SG", 0)])
                    P.add("dve", lambda e: e.memset(DUM[:], 0.0), writes=["DUM"])
                    P.add("dve", lambda e: e.tensor_tensor(out=PD[:, 0:16], in0=SG[0][:, 0:16], in1=PM[0][:, 16:32],
                                                           op=ALU.subtract),
                          reads=[("SG", 0), ("PM", 0), "PD"], writes=["PD"])
                if t < NT - 1:
                    P.add("dve", lambda e, c=c: e.tensor_copy(out=PH[:, c, :], in_=PM[0][:, 512:528]),
                          reads=[("PM", 0)], writes=[("PH", c)])
                b2 = rot_p.next()
                P.add("pe", lambda e, c=c, b2=b2: e.matmul(ps[b2][:], lhsT=PW[:, (l * 2 + c) * 128:(l * 2 + c + 1) * 128],
                                                          rhs=PD[:], start=True, stop=True),
                      reads=["PW", "PD"], writes=[("ps", b2)])
                P.add("dve", lambda e, c=c, b2=b2: e.tensor_scalar(
                    out=AO[:, c, :], in0=ps[b2][:], scalar1=vcol(V_PS + l * 2 + c), scalar2=None, op0=ALU.mult),
                    reads=[("ps", b2), "VEC"], writes=[("AO", c)])
            sx = wload(wb["win"][l, 5], 2048, wkey)
            sbb = wload(wb["win"][l, 6], 2048, wkey)
            sc = wload(wb["win"][l, 7], 2048, wkey)
            for c in range(2):
                bx = proj(sx, c)
                bc = proj(sc, c)
                bb = proj(sbb, c)
                P.add("act", lambda e, bx=bx: e.copy(out=PM[1][:, 0:512], in_=ps[bx][:]), reads=[("ps", bx)],
                      writes=[("PM", 1)])
                if t == 0:
                    P.add("dve", lambda e: e.memset(PM[2][:, 0:2], 0.0), writes=[("PM", 2)])
                else:
                    P.add("dve", lambda e, c=c: e.tensor_copy(out=PM[2][:, 0:2], in_=CH[:, c, :]),
                          reads=[("CH", c)], writes=[("PM", 2)])
                P.add("dve", lambda e, bc=bc: e.tensor_tensor(out=PM[2][:, 2:514], in0=ps[bc][:], in1=PM[1][:, 0:512],
                                                              op=ALU.mult),
                      reads=[("ps", bc), ("PM", 1)], writes=[("PM", 2)])
                if t < NT - 1:
                    P.add("dve", lambda e, c=c: e.tensor_copy(out=CH[:, c, :], in_=PM[2][:, 512:514]),
                          reads=[("PM", 2)], writes=[("CH", c)])
                cw = lambda j, c=c: vcol(V_CW + (l * 3 + j) * 2 + c)
                P.add("dve", lambda e, c=c, cw=cw: e.tensor_scalar(
                    out=PM[0][:, 0:512], in0=PM[2][:, 2:514], scalar1=cw(2), scalar2=vcol(V_CB + l * 2 + c),
                    op0=ALU.mult, op1=ALU.add),
                    reads=[("PM", 2), "VEC"], writes=[("PM", 0)])
                P.add("dve", lambda e: e.memset(DUM[:], 0.0), writes=["DUM"])
                P.add("dve", lambda e, cw=cw: e.scalar_tensor_tensor(
                    out=PM[1][:, 0:512], in0=PM[2][:, 1:513], scalar=cw(1), in1=PM[0][:, 0:512], op0=ALU.mult, op1=ALU.add),
                    reads=[("PM", 2), ("PM", 0), "VEC"], writes=[("PM", 1)])
                P.add("dve", lambda e: e.memset(DUM[:], 0.0), writes=["DUM"])
                P.add("dve", lambda e, cw=cw: e.scalar_tensor_tensor(
                    out=PM[0][:, 0:512], in0=PM[2][:, 0:512], scalar=cw(0), in1=PM[1][:, 0:512], op0=ALU.mult, op1=ALU.add),
                    reads=[("PM", 2), ("PM", 1), "VEC"], writes=[("PM", 0)])
                P.add("dve", lambda e: e.memset(DUM[:], 0.0), writes=["DUM"])
                P.add("dve", lambda e, c=c, bb=bb: e.tensor_tensor(out=CO[:, c, :], in0=ps[bb][:], in1=PM[0][:, 0:512],
                                                                    op=ALU.mult),
                      reads=[("ps", bb), ("PM", 0)], writes=[("CO", c)])
            nkb = 4 * (t + 1)
            for c in range(4):
                for hd in range(2):
                    P.add("pool", lambda e, hd=hd: e.memset(R32[hd][:], 0.0), writes=[("R32", hd)])
                kbs = list(reversed(range(nkb)))
                slots = {}

                def geom(kb):
                    j = kb - 4 * t
                    return j, (128 * j if j > 0 else 0)

                def zbank(kb, hd):
                    return hd * 3 + (nkb - 1 - kb) % 3

                def st_z(kb, c=c):
                    j, c0 = geom(kb)
                    kt_ap = KT_(c, kb * 128, (kb + 1) * 128)
                    for hd in range(2):
                        hb = 64 * hd
                        zb = zbank(kb, hd)
                        P.add("pe", lambda e, zb=zb, c0=c0, kt_ap=kt_ap, hb=hb: e.matmul(
                            ps[zb][:, c0:T], lhsT=kt_ap[hb:hb + 64, :], rhs=QS_(c)[hb:hb + 64, c0:T], start=True, stop=True),
                            reads=[("KT", c, kb // 4), ("QS", c)], writes=[("ps", zb)])

                def st_sp(kb):
                    j, c0 = geom(kb)
                    for hd in range(2):
                        zb = zbank(kb, hd)
                        ke = rot_sg.next()
                        ks = rot_asp.next()
                        slots[(kb, hd)] = ks
                        P.add("act", lambda e, zb=zb, c0=c0, ke=ke: e.activation(out=SG[ke][:, c0:T], in_=ps[zb][:, c0:T],
                                                                                  func=AF.Exp),
                              reads=[("ps", zb)], writes=[("SG", ke)])
                        P.add("act", lambda e, c0=c0, ke=ke, ks=ks: e.activation(out=ASP[ks][:, c0:T], in_=SG[ke][:, c0:T],
                                                                                  func=AF.Ln, bias=1.0),
                              reads=[("SG", ke)], writes=[("ASP", ks)])
                        if j >= 0:
                            P.add("dve", lambda e, c0=c0, ks=ks: e.tensor_tensor(
                                out=ASP[ks][:, c0:c0 + 128], in0=ASP[ks][:, c0:c0 + 128], in1=MSK, op=ALU.mult),
                                reads=[("ASP", ks), "CB"], writes=[("ASP", ks)])

                def st_c(kb, first, c=c):
                    j, c0 = geom(kb)
                    for hd in range(2):
                        zb = zbank(kb, hd)
                        ks = slots[(kb, hd)]
                        P.add("pe", lambda e, zb=zb, c0=c0, ks=ks: e.matmul(
                            ps[zb][:, c0:T], lhsT=NTRI, rhs=ASP[ks][:, c0:T], start=False, stop=first,
                            skip_group_check=True),
                            reads=[("ASP", ks), "CB"], writes=[("ps", zb)])
                        if not first:
                            rbi = 2 * hd + (kb + 1) % 2
                            P.add("pe", lambda e, zb=zb, c0=c0, rbi=rbi: e.matmul(
                                ps[zb][:, c0:T], lhsT=ONES1[:], rhs=RB[rbi][:, c0:T], start=False, stop=True,
                                skip_group_check=True),
                                reads=[("RB", rbi), "ONES1"], writes=[("ps", zb)])

                def st_w(kb):
                    j, c0 = geom(kb)
                    for hd in range(2):
                        zb = zbank(kb, hd)
                        kw = rot_aw.next()
                        slots[(kb, hd, "w")] = kw
                        P.add("act", lambda e, zb=zb, c0=c0, kw=kw: e.activation(out=AW[kw][:, c0:T], in_=ps[zb][:, c0:T],
                                                                                  func=AF.Exp),
                              reads=[("ps", zb)], writes=[("AW", kw)])
                        if j >= 0:
                            P.add("dve", lambda e, c0=c0, kw=kw: e.tensor_tensor(
                                out=AW[kw][:, c0:c0 + 128], in0=AW[kw][:, c0:c0 + 128], in1=MSK, op=ALU.mult),
                                reads=[("AW", kw), "CB"], writes=[("AW", kw)])

                def st_av(kb, first, c=c):
                    j, c0 = geom(kb)
                    for hd in range(2):
                        ob = 6 + hd
                        kw = slots[(kb, hd, "w")]
                        P.add("pe", lambda e, ob=ob, c0=c0, kw=kw: e.matmul(
                            ps[ob][:, c0:T], lhsT=V_(kb)[:, c * 128:(c + 1) * 128], rhs=AW[kw][:, c0:T], start=first,
                            stop=(kb == 0), skip_group_check=True),
                            reads=[("V", kb), ("AW", kw)], writes=[("ps", ob)])

                def st_r(kb):
                    j, c0 = geom(kb)
                    jn, c0n = geom(kb - 1)
                    for hd in range(2):
                        ks = slots[(kb, hd)]
                        P.add("pool", lambda e, c0=c0, ks=ks, hd=hd: e.tensor_tensor(
                            out=R32[hd][:, c0:T], in0=R32[hd][:, c0:T], in1=ASP[ks][:, c0:T], op=ALU.add),
                            reads=[("R32", hd), ("ASP", ks)], writes=[("R32", hd)])
                    for hd in range(2):
                        rbi = 2 * hd + kb % 2
                        P.add("dve", lambda e, c0n=c0n, hd=hd, rbi=rbi: e.tensor_copy(out=RB[rbi][:, c0n:T], in_=R32[hd][:, c0n:T]),
                              reads=[("R32", hd)], writes=[("RB", rbi)])

                nb = len(kbs)
                st_z(kbs[0])
                st_sp(kbs[0])
                if kbs[0] > 0:
                    st_r(kbs[0])
                st_c(kbs[0], True)
                if nb > 1:
                    st_z(kbs[1])
                for i, kb in enumerate(kbs):
                    if i + 1 < nb:
                        st_sp(kbs[i + 1])
                        if kbs[i + 1] > 0:
                            st_r(kbs[i + 1])
                        st_c(kbs[i + 1], False)
                    if i + 2 < nb:
                        st_z(kbs[i + 2])
                    st_w(kb)
                    st_av(kb, i == 0)
                for hd in range(2):
                    hb = 64 * hd
                    P.add("dve", lambda e, hd=hd, hb=hb, c=c: e.tensor_copy(out=BO[hb:hb + 64, c, :], in_=ps[6 + hd][hb:hb + 64, :]),
                          reads=[("ps", 6 + hd)], writes=[("BO", c)])
            for n in range(DC):
                sA = wload(wb["wgb"][l, n, 0], 2048, ("W", "wgb", l))
                sB = wload(wb["wgb"][l, n, 1], 2048, ("W", "wgb", l))
                gb = [rot_p.next() for _ in range(3)]
                bb_ = [rot_p.next() for _ in range(3)]
                for gi in range(3):
                    sl, off = (sA, gi * 8) if gi < 2 else (sB, 0)
                    for kc in range(DC):
                        P.add("pe", lambda e, kc=kc, sl=sl, off=off, bk=gb[gi]: e.matmul(
                            ps[bk][:], lhsT=WB[sl][:, (off + kc) * 128:(off + kc + 1) * 128], rhs=rhsHN(kc),
                            start=(kc == 0), stop=(kc == DC - 1)),
                            reads=[("WB", sl), ("HN", kc, hh)], writes=[("ps", gb[gi])])
                for bi, (off, nk, src, key) in enumerate(((8, 2, AO, "AO"), (10, 4, BO, "BO"), (14, 2, CO, "CO"))):
                    for kc in range(nk):
                        P.add("pe", lambda e, kc=kc, off=off, nk=nk, src=src, bk=bb_[bi], sB=sB: e.matmul(
                            ps[bk][:], lhsT=WB[sB][:, (off + kc) * 128:(off + kc + 1) * 128], rhs=src[:, kc, :],
                            start=(kc == 0), stop=(kc == nk - 1)),
                            reads=[("WB", sB), (key, kc)], writes=[("ps", bb_[bi])])
                ks = [rot_sg.next() for _ in range(3)]
                for gi in range(3):
                    P.add("act", lambda e, gi=gi, k=ks[gi], bk=gb[gi], n=n: e.activation(
                        out=SG[k][:], in_=ps[bk][:], func=AF.Sigmoid, bias=vcol(V_BG + (l * 3 + gi) * 8 + n)),
                        reads=[("ps", gb[gi]), "VEC"], writes=[("SG", ks[gi])])
                for gi in range(3):
                    P.add("dve", lambda e, k=ks[gi], bk=bb_[gi]: e.tensor_tensor(
                        out=SG[k][:], in0=ps[bk][:], in1=SG[k][:], op=ALU.mult),
                        reads=[("ps", bb_[gi]), ("SG", ks[gi])], writes=[("SG", ks[gi])])
                P.add("pool", lambda e, k0=ks[0], k1=ks[1]: e.tensor_tensor(out=SG[k0][:], in0=SG[k0][:], in1=SG[k1][:],
                                                                             op=ALU.add),
                      reads=[("SG", ks[0]), ("SG", ks[1])], writes=[("SG", ks[0])])
                P.add("pool", lambda e, k0=ks[0], k2=ks[2], n=n: e.tensor_tensor(
                    out=HN[:, n, mh * T:(mh + 1) * T], in0=SG[k0][:], in1=SG[k2][:], op=ALU.add),
                    reads=[("SG", ks[0]), ("SG", ks[2])], writes=[("HN", n, mh)])
            pend = None
            for i in range(4):
                s = wload(wb["wo"][l, i], 2048, ("W", "wo", l))
                for j in range(2):
                    n = 2 * i + j
                    bk = rot_p.next()
                    for kc in range(DC):
                        P.add("pe", lambda e, kc=kc, s=s, j=j, bk=bk: e.matmul(
                            ps[bk][:], lhsT=WB[s][:, (j * 8 + kc) * 128:(j * 8 + kc + 1) * 128],
                            rhs=HN[:, kc, mh * T:(mh + 1) * T], start=(kc == 0), stop=(kc == DC - 1)),
                            reads=[("WB", s), ("HN", kc, mh)], writes=[("ps", bk)])
                    if pend is not None:
                        pend()
                    pend = evac_h_and_stats(bk, n, None)
            pend()
            postnorm_update(V_G + (3 * L + l) * 8, t, 1.0)

        for s_ in range(NS):
            for c in range(DC):
                P.add("sp", lambda e, c=c, s_=s_: e.dma_start(out=X[:, c, :], in_=xT[s_, c]),
                      reads=(WKEYS if s_ == 0 else []), writes=[("X", c, t) for t in range(NT)], dma_key="X%d" % c)
            for l in range(NL):
                if "ffn1" in parts:
                    ffn(l, 0, 0)
                    ffn(l, 0, 1)
                fence()
                if "mix" in parts:
                    for t in range(NT):
                        mix(l, t)
                fence()
                if "ffn2" in parts:
                    ffn(l, 1, 0)
                    ffn(l, 1, 1)
            for c in range(DC):
                P.add("sp", lambda e, c=c, s_=s_: e.dma_start(out=yT[s_, c], in_=X[:, c, :]),
                      reads=[("X", c, t) for t in range(NT)], writes=[("Y", s_, c)], dma_key="X%d" % c)
        P.add("sp", lambda e: None, reads=[("Y", s_, c) for s_ in range(NS) for c in range(DC)])
        P.emit(nc, st)
    return nc, len(P.ops)


def _kcp(w):
    K, M = w.shape
    return w.reshape(K // 128, 128, M // 128, 128).transpose(2, 1, 0, 3)


def pack_weights(inp):
    f = np.float32
    W = {}
    wgu = np.empty((L, 2, FC, 128, 16, 128), f)
    wd = np.empty((L, 2, DC, 2, 128, 11, 128), f)
    win = np.empty((L, 8, 128, 2, 8, 128), f)
    wv = np.empty((L, 2, 128, 4, 512), f)
    wgb = np.empty((L, DC, 2, 128, 16, 128), f)
    wo = np.empty((L, 4, 128, 2, 8, 128), f)
    cols = np.concatenate([np.arange(768, 1280), np.arange(256, 768), np.arange(0, 256), np.arange(1792, 2560)])
    for l in range(L):
        for fi, pre in enumerate(("ffn1", "ffn2")):
            wgu[l, fi, :, :, 0:8] = _kcp(np.asarray(inp[pre + "_w_gate"][l]))
            wgu[l, fi, :, :, 8:16] = _kcp(np.asarray(inp[pre + "_w_up"][l]))
            d = _kcp(np.asarray(inp[pre + "_w_down"][l]))
            wd[l, fi] = d.reshape(DC, 128, 2, 11, 128).transpose(0, 2, 1, 3, 4)
        wi = np.asarray(inp["w_in"][l])
        p = _kcp(wi[:, cols])
        win[l] = p.reshape(8, 2, 128, 8, 128).transpose(0, 2, 1, 3, 4)
        v = wi[:, 1280:1792].reshape(2, 4, 128, 512)
        wv[l] = v.transpose(0, 2, 1, 3)
        g = [_kcp(wi[:, 2560 + i * 1024: 2560 + (i + 1) * 1024]) for i in range(3)]
        wgb[l, :, 0, :, 0:8] = g[0]
        wgb[l, :, 0, :, 8:16] = g[1]
        wgb[l, :, 1, :, 0:8] = g[2]
        wgb[l, :, 1, :, 8:10] = _kcp(np.asarray(inp["w_br_pool"][l]))
        wgb[l, :, 1, :, 10:14] = _kcp(np.asarray(inp["w_br_sb"][l]))
        wgb[l, :, 1, :, 14:16] = _kcp(np.asarray(inp["w_br_conv"][l]))
        o = _kcp(np.asarray(inp["w_out"][l]))
        wo[l] = o.reshape(4, 2, 128, 8, 128).transpose(0, 2, 1, 3, 4)
    W["wgu"] = wgu.reshape(L, 2, FC, 128, 2048)
    W["wd"] = wd.reshape(L, 2, DC, 2, 128, 1408)
    W["win"] = win.reshape(L, 8, 128, 2048)
    W["wv"] = wv.reshape(L, 2, 128, 2048)
    W["wgb"] = wgb.reshape(L, DC, 2, 128, 2048)
    W["wo"] = wo.reshape(L, 4, 128, 2048)
    vec = np.zeros((128, NV), f)
    gn = ("ffn1_pre_g", "ffn1_post_g", "mix_pre_g", "mix_post_g", "ffn2_pre_g", "ffn2_post_g")
    for gi, name in enumerate(gn):
        g_ = np.asarray(inp[name])
        vec[:, V_G + gi * L * 8: V_G + (gi + 1) * L * 8] = g_.reshape(L, 8, 128).transpose(2, 0, 1).reshape(128, L * 8)
    bg = np.asarray(inp["b_gate"])
    vec[:, V_BG:V_BG + L * 3 * 8] = bg.reshape(L, 3, 8, 128).transpose(3, 0, 1, 2).reshape(128, L * 24)
    vec[:, V_PS:V_PS + L * 2] = np.asarray(inp["pool_scale"]).reshape(L, 2, 128).transpose(2, 0, 1).reshape(128, L * 2)
    vec[:, V_CW:V_CW + L * 6] = np.asarray(inp["conv_w"]).reshape(L, 3, 2, 128).transpose(3, 0, 1, 2).reshape(128, L * 6)
    vec[:, V_CB:V_CB + L * 2] = np.asarray(inp["conv_b"]).reshape(L, 2, 128).transpose(2, 0, 1).reshape(128, L * 2)
    wins = (2, 4, 8, 16)
    for c in range(2):
        for hlf in range(2):
            w_ = wins[2 * c + hlf]
            vec[64 * hlf:64 * hlf + 64, V_IW + c] = 1.0 / w_
            for t in range(16):
                vec[64 * hlf:64 * hlf + 64, V_IC + c * 16 + t] = 1.0 / min(t + 1, w_)
    W["vec"] = vec
    cst = np.zeros((128, 384), f)
    i = np.arange(128)
    cst[:, 0:128] = (i[:, None] >= i[None, :]).astype(f)
    cst[:, 128:256] = (i[None, :] > i[:, None]).astype(f)
    cst[:, 256:384] = -cst[:, 0:128]
    W["cst"] = cst
    pw = np.zeros((128, L, 2, 128), f)
    pool_w = np.asarray(inp["pool_w"])
    for l in range(L):
        for c in range(2):
            for hlf in range(2):
                pw[64 * hlf:64 * hlf + 64, l, c, 64 * hlf:64 * hlf + 64] = pool_w[l, 2 * c + hlf]
    W["poolw"] = pw.reshape(128, L * 2 * 128)
    return W


_NC_CACHE = {}


def kernel(**inputs):
    x = np.asarray(inputs["x"], dtype=np.float32)
    B = x.shape[0]
    W = pack_weights(inputs)
    key = (SEQ_PER_CORE, L)
    if key not in _NC_CACHE:
        _NC_CACHE[key] = build_program(SEQ_PER_CORE, L)[0]
    nc = _NC_CACHE[key]
    in_maps = []
    for c in range(NCORE):
        xs = x[c * SEQ_PER_CORE:(c + 1) * SEQ_PER_CORE]
        xT = np.ascontiguousarray(xs.transpose(0, 2, 1)).reshape(SEQ_PER_CORE, DC, 128, S)
        m = {"xT": xT}
        m.update(W)
        in_maps.append(m)
    res = run_bass_kernel_spmd(nc, in_maps, core_ids=list(range(NCORE)))
    out = np.empty((B, S, D), np.float32)
    for c in range(NCORE):
        yT = np.asarray(res.results[c]["yT"]).reshape(SEQ_PER_CORE, D, S)
        out[c * SEQ_PER_CORE:(c + 1) * SEQ_PER_CORE] = yT.transpose(0, 2, 1)
    return out
```

```python
import numpy as np
from contextlib import ExitStack
import concourse.bass as bass
import concourse.mybir as mybir
from concourse.bass_utils import run_bass_kernel_spmd

F32 = mybir.dt.float32
BF16 = mybir.dt.bfloat16
AF = mybir.ActivationFunctionType
ALU = mybir.AluOpType

L = 4
D = 1024
S = 2048
T = 512
NT = 4
DC = 8
FC = 22
NCORE = 8
SEQ_PER_CORE = 4
EPS = 1e-6
NV = 362
V_G, V_BG, V_PS, V_CW, V_CB, V_IW, V_IC = 0, 192, 288, 296, 320, 328, 330

EPOCH = 30000
NDUMMY = 2
DUMBANK = 6
NDUMMY = 2


class Op:
    __slots__ = ("eng", "fn", "reads", "writes", "dma_key", "seq", "waits", "signal", "rank", "clock", "res")

    def __init__(self, eng, fn, reads, writes, dma_key):
        self.eng = eng
        self.fn = fn
        self.reads = reads
        self.writes = writes
        self.dma_key = dma_key
        self.seq = 0
        self.waits = []
        self.signal = False
        self.rank = 0
        self.clock = None
        self.res = None


class Prog:
    ENGS = ("pe", "act", "dve", "pool", "sp")

    def __init__(self):
        self.ops = []

    def add(self, eng, fn, reads=(), writes=(), dma_key=None):
        reads = tuple(reads)
        writes = tuple(writes) + tuple(k for k in reads if isinstance(k, tuple) and k[0] == "ps")
        self.ops.append(Op(eng, fn, reads, writes, dma_key))

    def analyze(self):
        last_writer = {}
        readers = {}
        seqctr = {}
        known = {e: {} for e in self.ENGS}
        by_res = {}
        for op in self.ops:
            res = ("dma:" + str(op.dma_key)) if op.dma_key is not None else op.eng
            seqctr[res] = seqctr.get(res, 0) + 1
            op.seq = seqctr[res]
            op.res = res
            by_res[(res, op.seq)] = op
            deps = set()
            for r in op.reads:
                w = last_writer.get(r)
                if w is not None:
                    deps.add(w)
            for w_ in op.writes:
                w = last_writer.get(w_)
                if w is not None:
                    deps.add(w)
                for rd in readers.get(w_, ()):
                    deps.add(rd)
            deps.discard(op)
            kn = known[op.eng]
            need = {}
            for d in deps:
                if d.res == op.eng and op.dma_key is None and op.eng == "pe":
                    continue
                if kn.get(d.res, 0) >= d.seq:
                    continue
                if need.get(d.res, 0) < d.seq:
                    need[d.res] = d.seq
            items = sorted(need.items())
            final = []
            for r, v in items:
                implied = False
                for r2, v2 in items:
                    if (r2, v2) == (r, v):
                        continue
                    if by_res[(r2, v2)].clock.get(r, 0) >= v:
                        implied = True
                        break
                if not implied:
                    final.append((r, v))
            op.waits = final
            for r, v in final:
                d = by_res[(r, v)]
                d.signal = True
                for rr, vv in d.clock.items():
                    if kn.get(rr, 0) < vv:
                        kn[rr] = vv
            ck = dict(kn)
            ck[res] = op.seq
            op.clock = ck
            for r in op.reads:
                readers.setdefault(r, []).append(op)
            for w_ in op.writes:
                last_writer[w_] = op
                readers[w_] = []
        rk = {}
        for op in self.ops:
            if op.dma_key is None:
                if op.signal:
                    rk[op.res] = rk.get(op.res, 0) + 1
                    op.rank = rk[op.res]
            else:
                op.rank = op.seq
        self.nsig = rk
        self.by_res = by_res

    def emit(self, nc, stack):
        self.analyze()
        sems = {}
        for e in self.ENGS:
            n = self.nsig.get(e, 0)
            for k in range((n + EPOCH - 1) // EPOCH):
                sems[(e, k)] = stack.enter_context(nc.semaphore("s_%s_%d" % (e, k)))
        dkeys = sorted({op.res for op in self.ops if op.dma_key is not None})
        for i, r in enumerate(dkeys):
            sems[(r, 0)] = stack.enter_context(nc.semaphore("sd_%d" % i))
        by_res = self.by_res

        def sem_val(r, v):
            d = by_res[(r, v)]
            if d.dma_key is not None:
                return sems[(r, 0)], 16 * d.rank
            k = (d.rank - 1) // EPOCH
            return sems[(r, k)], (d.rank - 1) % EPOCH + 1

        per_eng = {e: [] for e in self.ENGS}
        for op in self.ops:
            per_eng[op.eng].append(op)
        block = stack.enter_context(nc.Block())

        def run(e, ops):
            def body(eng):
                for op in ops:
                    for r, v in op.waits:
                        s, val = sem_val(r, v)
                        eng.wait_ge(s, val)
                    ins = op.fn(eng)
                    if ins is None:
                        continue
                    if op.dma_key is not None:
                        ins.then_inc(sems[(op.res, 0)], 16)
                    elif op.signal:
                        k = (op.rank - 1) // EPOCH
                        ins.then_inc(sems[(e, k)], 1)
            return body

        block.tensor(run("pe", per_eng["pe"]))
        block.scalar(run("act", per_eng["act"]))
        block.vector(run("dve", per_eng["dve"]))
        block.gpsimd(run("pool", per_eng["pool"]))
        block.sync(run("sp", per_eng["sp"]))


class Rot:
    def __init__(self, items):
        self.items = list(items)
        self.i = 0

    def next(self):
        v = self.items[self.i % len(self.items)]
        self.i += 1
        return v


def build_program(NS=SEQ_PER_CORE, NL=L, parts=("ffn1", "mix", "ffn2")):
    nc = bass.Bass("TRN2", target_bir_lowering=False)

    def din(name, shape, dt=F32):
        return nc.dram_tensor(name, list(shape), dt, kind="ExternalInput").ap()

    def dscr(name, shape):
        return nc.dram_tensor(name, list(shape), BF16, kind="Internal").ap()

    xT = din("xT", [NS, DC, 128, S])
    yT = nc.dram_tensor("yT", [NS, DC, 128, S], F32, kind="ExternalOutput").ap()
    wshapes = {
        "wgu": [L, 2, FC, 128, 16 * 128],
        "wd": [L, 2, DC, 2, 128, 11 * 128],
        "win": [L, 8, 128, 16 * 128],
        "wv": [L, 2, 128, 4 * 512],
        "wgb": [L, DC, 2, 128, 16 * 128],
        "wo": [L, 4, 128, 16 * 128],
    }
    wf = {k: din(k, v) for k, v in wshapes.items()}
    wb = {k: dscr(k + "_b", v) for k, v in wshapes.items()}
    vec_d = din("vec", [128, NV])
    cst_d = din("cst", [128, 384])
    pw_d = din("poolw", [128, L * 2 * 128])

    P = Prog()
    with ExitStack() as st:
        def sb(name, shape, dt):
            return st.enter_context(nc.sbuf_tensor(name, list(shape), dt))

        X = sb("X", [128, DC, S], F32)
        HN = sb("HN", [128, DC, 2 * T], BF16)
        RA = sb("RA", [128, FC * 2 * T], BF16)
        H = sb("H", [128, DC, T], F32)
        PM = [sb("PM%d" % i, [128, 528], F32) for i in range(3)]
        PH = sb("PH", [128, 2, 16], F32)
        CH = sb("CH", [128, 2, 2], F32)
        PD = sb("PD", [128, T], BF16)
        AO = sb("AO", [128, 2, T], BF16)
        BO = sb("BO", [128, 4, T], BF16)
        CO = sb("CO", [128, 2, T], BF16)
        SG = [sb("SG%d" % i, [128, T], F32) for i in range(4)]
        ASP = [sb("ASP%d" % i, [128, T], BF16) for i in range(4)]
        AW = [sb("AW%d" % i, [128, T], BF16) for i in range(4)]
        R32 = [sb("R32_%d" % i, [128, T], F32) for i in range(2)]
        RB = [sb("RB%d" % i, [128, T], BF16) for i in range(4)]
        WB = [sb("WB%d" % i, [128, 2048], BF16) for i in range(4)]
        CB = sb("CB", [128, 384], BF16)
        PW = sb("PW", [128, L * 2 * 128], BF16)
        ONESM = sb("ONESM", [128, 128], BF16)
        ONES1 = sb("ONES1", [128, 128], BF16)
        VEC = sb("VEC", [128, NV], F32)
        RSTD = [sb("RSTD%d" % i, [128, T], F32) for i in range(2)]
        SQ = [sb("SQ%d" % i, [128, T], BF16) for i in range(2)]
        DUM = sb("DUM", [128, 2], F32)
        ps = [st.enter_context(nc.psum_tensor("ps%d" % i, [128, T], F32)) for i in range(8)]

        TRI = CB[:, 0:128]
        MSK = CB[:, 128:256]
        NTRI = CB[:, 256:384]

        def A_(m, tt):
            return RA[:, m * 1024 + tt * 512: m * 1024 + (tt + 1) * 512]

        def KT_(c, a, b):
            return RA[:, c * S + a: c * S + b]

        def V_(kb):
            return RA[:, 8192 + kb * 512: 8192 + (kb + 1) * 512]

        def QS_(c):
            return RA[:, 16384 + c * 512: 16384 + (c + 1) * 512]

        def NQ_(c):
            return RA[:, 18432 + c * 512: 18432 + (c + 1) * 512]

        A_KEYS = [("A", m, tt) for m in range(FC) for tt in range(2)]
        M_KEYS = ([("KT", c, t) for c in range(4) for t in range(NT)] + [("V", kb) for kb in range(16)]
                  + [("QS", c) for c in range(4)] + [("NQ", c) for c in range(4)])

        def vcol(i):
            return VEC[:, i:i + 1]

        rot_rstd = Rot([0, 1])
        rot_sq = Rot([0, 1])
        rot_wb = Rot([0, 1, 2, 3])
        rot_sg = Rot([0, 1, 2, 3])

        P.add("sp", lambda e: e.dma_start(out=VEC[:], in_=vec_d), writes=["VEC"], dma_key="VEC")
        Hflat = H[:, :, :]
        P.add("sp", lambda e: e.dma_start(out=H[:, 0, 0:384], in_=cst_d), writes=[("H", 0)], dma_key="H0")
        P.add("sp", lambda e: e.dma_start(out=H[:, 2, :], in_=pw_d[:, 0:512]), writes=[("H", 2)], dma_key="H2")
        P.add("sp", lambda e: e.dma_start(out=H[:, 3, :], in_=pw_d[:, 512:1024]), writes=[("H", 3)], dma_key="H3")
        P.add("dve", lambda e: e.tensor_copy(out=CB[:], in_=H[:, 0, 0:384]), reads=[("H", 0)], writes=["CB"])
        P.add("dve", lambda e: e.tensor_copy(out=PW[:, 0:512], in_=H[:, 2, :]), reads=[("H", 2)], writes=["PW"])
        P.add("dve", lambda e: e.tensor_copy(out=PW[:, 512:1024], in_=H[:, 3, :]), reads=[("H", 3)], writes=["PW"])
        P.add("dve", lambda e: e.memset(ONESM[:], 1.0 / D), writes=["ONESM"])
        P.add("dve", lambda e: e.memset(ONES1[:], -1.0), writes=["ONES1"])
        cvi = [0]

        WKEYS = []

        def convert(name, idx):
            WKEYS.append(("W", name) + tuple(idx))
            src = wf[name]
            dst = wb[name]
            for i in idx:
                src = src[i]
                dst = dst[i]
            k = cvi[0]
            cvi[0] += 1
            P.add("pool", lambda e, s_=src, d_=dst: e.dma_start(out=d_, in_=s_),
                  writes=[("W", name) + tuple(idx)], dma_key="cv%d" % k)

        for l in range(NL):
            convert("wgu", (l, 0))
            convert("wd", (l, 0))
            convert("win", (l,))
            convert("wv", (l,))
            convert("wgb", (l,))
            convert("wo", (l,))
            convert("wgu", (l, 1))
            convert("wd", (l, 1))

        def wload(src, ncols, rkey):
            s = rot_wb.next()
            P.add("sp", lambda e, s=s, src=src: e.dma_start(out=WB[s][:, 0:ncols], in_=src),
                  reads=[rkey], writes=[("WB", s)], dma_key="WB%d" % s)
            return s

        def rstd_finish(r):
            P.add("act", lambda e: e.activation(out=RSTD[r][:], in_=ps[6][:], func=AF.Ln, bias=EPS),
                  reads=[("ps", 6)], writes=[("RSTD", r)])
            P.add("act", lambda e: e.activation(out=RSTD[r][:], in_=RSTD[r][:], func=AF.Exp, scale=-0.5),
                  reads=[("RSTD", r)], writes=[("RSTD", r)])

        def prenorm(gcol, tile, hh):
            a, b = tile * T, (tile + 1) * T
            r = rot_rstd.next()
            for c in range(DC):
                q = rot_sq.next()
                P.add("act", lambda e, c=c, q=q: e.activation(out=SQ[q][:], in_=X[:, c, a:b], func=AF.Square),
                      reads=[("X", c, tile)], writes=[("SQ", q)])
                P.add("pe", lambda e, c=c, q=q: e.matmul(ps[6][:], lhsT=ONESM[:], rhs=SQ[q][:], start=(c == 0),
                                                         stop=(c == DC - 1)),
                      reads=[("SQ", q), "ONESM"], writes=[("ps", 6)])
            rstd_finish(r)
            for c in range(DC):
                P.add("dve", lambda e, c=c: e.scalar_tensor_tensor(
                    out=HN[:, c, hh * T:(hh + 1) * T], in0=X[:, c, a:b], scalar=vcol(gcol + c), in1=RSTD[r][:],
                    op0=ALU.mult, op1=ALU.mult),
                    reads=[("X", c, tile), ("RSTD", r), "VEC"], writes=[("HN", c, hh)])

        def postnorm_update(gcol, tile, factor):
            a, b = tile * T, (tile + 1) * T
            r = rot_rstd.next()
            rstd_finish(r)
            pend = None
            for c in range(DC):
                k = rot_sg.next()
                P.add("dve", lambda e, c=c, k=k: e.scalar_tensor_tensor(
                    out=SG[k][:], in0=H[:, c, :], scalar=vcol(gcol + c), in1=RSTD[r][:], op0=ALU.mult, op1=ALU.mult),
                    reads=[("H", c), ("RSTD", r), "VEC"], writes=[("SG", k)])

                def upd(c=c, k=k):
                    P.add("dve", lambda e: e.scalar_tensor_tensor(
                        out=X[:, c, a:b], in0=SG[k][:], scalar=float(factor), in1=X[:, c, a:b], op0=ALU.mult,
                        op1=ALU.add),
                        reads=[("SG", k), ("X", c, tile)], writes=[("X", c, tile)])
                if pend is not None:
                    pend()
                pend = upd
            pend()

        def evac_h_and_stats(bank, n, pending):
            q = rot_sq.next()
            P.add("dve", lambda e: e.tensor_copy(out=H[:, n, :], in_=ps[bank][:]), reads=[("ps", bank)],
                  writes=[("H", n)])
            P.add("act", lambda e: e.activation(out=SQ[q][:], in_=H[:, n, :], func=AF.Square),
                  reads=[("H", n)], writes=[("SQ", q)])

            def stat():
                P.add("pe", lambda e: e.matmul(ps[6][:], lhsT=ONESM[:], rhs=SQ[q][:], start=(n == 0),
                                               stop=(n == DC - 1)),
                      reads=[("SQ", q), "ONESM"], writes=[("ps", 6)])
            return stat

        rot_g = Rot([0, 1])
        rot_u = Rot([2, 3])
        rot_d = Rot([4, 5])

        def ffn(l, f, hf):
            gpre = V_G + ((0 if f == 0 else 4) * L + l) * 8
            gpost = V_G + ((1 if f == 0 else 5) * L + l) * 8
            for tt in range(2):
                prenorm(gpre, 2 * hf + tt, tt)
            for m in range((FC if "mlim" not in parts else 2) if "nogu" not in parts else 0):
                s = wload(wb["wgu"][l, f, m], 2048, ("W", "wgu", l, f))
                for tt in range(2):
                    g = rot_g.next()
                    u = rot_u.next()
                    for kc in range(DC):
                        P.add("pe", lambda e, kc=kc, s=s, g=g, tt=tt: e.matmul(
                            ps[g][:], lhsT=WB[s][:, kc * 128:(kc + 1) * 128], rhs=HN[:, kc, tt * T:(tt + 1) * T],
                            start=(kc == 0), stop=(kc == DC - 1)),
                            reads=[("WB", s), ("HN", kc, tt)], writes=[("ps", g)])
                    for kc in range(DC):
                        P.add("pe", lambda e, kc=kc, s=s, u=u, tt=tt: e.matmul(
                            ps[u][:], lhsT=WB[s][:, (8 + kc) * 128:(9 + kc) * 128], rhs=HN[:, kc, tt * T:(tt + 1) * T],
                            start=(kc == 0), stop=(kc == DC - 1)),
                            reads=[("WB", s), ("HN", kc, tt)], writes=[("ps", u)])
                    k = rot_sg.next()
                    P.add("act", lambda e, g=g, k=k: e.activation(out=SG[k][:], in_=ps[g][:], func=AF.Silu),
                          reads=[("ps", g)], writes=[("SG", k)])
                    P.add("dve", lambda e, u=u, k=k, m=m, tt=tt: e.tensor_tensor(
                        out=A_(m, tt), in0=ps[u][:], in1=SG[k][:], op=ALU.mult),
                        reads=[("ps", u), ("SG", k)], writes=[("A", m, tt)])
            for tt in range(2 if "nodown" not in parts else 0):
                pend = None
                for n in range(DC):
                    s0 = wload(wb["wd"][l, f, n, 0], 1408, ("W", "wd", l, f))
                    s1 = wload(wb["wd"][l, f, n, 1], 1408, ("W", "wd", l, f))
                    d = rot_d.next()
                    for kc in range(FC):
                        sl = s0 if kc < 11 else s1
                        P.add("pe", lambda e, kc=kc, sl=sl, d=d, tt=tt: e.matmul(
                            ps[d][:], lhsT=WB[sl][:, (kc % 11) * 128:(kc % 11 + 1) * 128], rhs=A_(kc, tt),
                            start=(kc == 0), stop=(kc == FC - 1)),
                            reads=[("WB", sl), ("A", kc, tt)], writes=[("ps", d)])
                    if pend is not None:
                        pend()
                    pend = evac_h_and_stats(d, n, None)
                pend()
                postnorm_update(gpost, 2 * hf + tt, 0.5)

        def fence():
            P.add("dve", lambda e: e.memset(DUM[:], 0.0), writes=A_KEYS + M_KEYS + ["DUM"])

        rot_p = Rot([0, 1, 2, 3, 4, 5, 7])
        rot_z = Rot([0, 1])
        rot_c = Rot([2, 3])
        rot_o = Rot([4, 5])
        rot_asp = Rot([0, 1, 2, 3])
        rot_aw = Rot([0, 1, 2, 3])
        rot_r = Rot([0, 1])

        def mix(l, t):
            hh = t % 2
            mh = 1 - hh
            a, b = t * T, (t + 1) * T
            prenorm(V_G + (2 * L + l) * 8, t, hh)

            def rhsHN(kc):
                return HN[:, kc, hh * T:(hh + 1) * T]

            def proj(slot, j):
                bk = rot_p.next()
                for kc in range(DC):
                    P.add("pe", lambda e, kc=kc: e.matmul(
                        ps[bk][:], lhsT=WB[slot][:, (j * 8 + kc) * 128:(j * 8 + kc + 1) * 128], rhs=rhsHN(kc),
                        start=(kc == 0), stop=(kc == DC - 1)),
                        reads=[("WB", slot), ("HN", kc, hh)], writes=[("ps", bk)])
                return bk

            wkey = ("W", "win", l)
            for i in range(2):
                s = wload(wb["win"][l, i], 2048, wkey)
                for j in range(2):
                    c = 2 * i + j
                    bk = proj(s, j)
                    P.add("act", lambda e, c=c, bk=bk: e.copy(out=KT_(c, a, b), in_=ps[bk][:]),
                          reads=[("ps", bk)], writes=[("KT", c, t)])
            for i in range(2):
                s = wload(wb["win"][l, 2 + i], 2048, wkey)
                for j in range(2):
                    c = 2 * i + j
                    bk = proj(s, j)
                    P.add("act", lambda e, c=c, bk=bk: e.mul(out=QS_(c), in_=ps[bk][:], mul=0.125),
                          reads=[("ps", bk)], writes=[("QS", c)])
            sv = [wload(wb["wv"][l, i], 2048, ("W", "wv", l)) for i in range(2)]
            for tb in range(4):
                bk = rot_p.next()
                for kc in range(DC):
                    P.add("pe", lambda e, kc=kc, bk=bk, tb=tb: e.matmul(
                        ps[bk][:], lhsT=HN[:, kc, hh * T + tb * 128: hh * T + (tb + 1) * 128],
                        rhs=WB[sv[kc // 4]][:, (kc % 4) * 512:(kc % 4 + 1) * 512],
                        start=(kc == 0), stop=(kc == DC - 1)),
                        reads=[("WB", sv[kc // 4]), ("HN", kc, hh)], writes=[("ps", bk)])
                P.add("dve", lambda e, bk=bk, tb=tb: e.tensor_copy(out=V_(4 * t + tb), in_=ps[bk][:]),
                      reads=[("ps", bk)], writes=[("V", 4 * t + tb)])
            s = wload(wb["win"][l, 4], 2048, wkey)
            for c in range(2):
                bk = proj(s, c)
                U, B1, B2 = PM[0], PM[1], PM[2]
                P.add("act", lambda e, bk=bk: e.copy(out=PM[0][:, 16:528], in_=ps[bk][:]), reads=[("ps", bk)],
                      writes=[("PM", 0)])
                if t == 0:
                    P.add("dve", lambda e: e.memset(PM[0][:, 0:16], 0.0), writes=[("PM", 0)])
                else:
                    P.add("dve", lambda e, c=c: e.tensor_copy(out=PM[0][:, 0:16], in_=PH[:, c, :]),
                          reads=[("PH", c)], writes=[("PM", 0)])
                n0 = 1 + 2 * c
                src, dst = 0, 1
                for k in range(1, n0 + 2):
                    sh = 1 << (k - 1)
                    lo = (1 << k) - 1
                    p0 = 0 if k <= n0 else 64
                    P.add("dve", lambda e, src=src, dst=dst, sh=sh, lo=lo, p0=p0: e.tensor_tensor(
                        out=PM[dst][p0:128, lo:528], in0=PM[src][p0:128, lo:528], in1=PM[src][p0:128, lo - sh:528 - sh],
                        op=ALU.add),
                        reads=[("PM", src)], writes=[("PM", dst)])
                    if k == 1:
                        src, dst = 1, 2
                    else:
                        src, dst = dst, src
                for hlf, bufi in ((0, 1), (1, 2)):
                    p0 = 64 * hlf
                    P.add("dve", lambda e, p0=p0, bufi=bufi, c=c: e.scalar_tensor_tensor(
                        out=PD[p0:p0 + 64, :], in0=PM[bufi][p0:p0 + 64, 16:528], scalar=VEC[p0:p0 + 64, V_IW + c:V_IW + c + 1],
                        in1=PM[0][p0:p0 + 64, 16:528], op0=ALU.mult, op1=ALU.subtract),
                        reads=[("PM", bufi), ("PM", 0), "VEC"], writes=["PD"])
                if t == 0:
                    for hlf, bufi in ((0, 1), (1, 2)):
                        p0 = 64 * hlf
                        P.add("dve", lambda e, p0=p0, bufi=bufi, c=c: e.tensor_tensor(
                            out=SG[0][p0:p0 + 64, 0:16], in0=PM[bufi][p0:p0 + 64, 16:32],
                            in1=VEC[p0:p0 + 64, V_IC + c * 16:V_IC + (c + 1) * 16], op=ALU.mult),
                            reads=[("PM", bufi), "VEC"], writes=[("SG", 0)])
                    P.add("dve", lambda e: e.memset(DUM[:], 0.0), writes=["DUM"])
                    P.add("dve", lambda e: e.tensor_tensor(out=PD[:, 0:16], in0=SG[0][:, 0:16], in1=PM[0][:, 16:32],
                                                           op=ALU.subtract),
                          reads=[("SG", 0), ("PM", 0), "PD"], writes=["PD"])
                if t < NT - 1:
                    P.add("dve", lambda e, c=c: e.tensor_copy(out=PH[:, c, :], in_=PM[0][:, 512:528]),
                          reads=[("PM", 0)], writes=[("PH", c)])
                b2 = rot_p.next()
                P.add("pe", lambda e, c=c, b2=b2: e.matmul(ps[b2][:], lhsT=PW[:, (l * 2 + c) * 128:(l * 2 + c + 1) * 128],
                                                          rhs=PD[:], start=True, stop=True),
                      reads=["PW", "PD"], writes=[("ps", b2)])
                P.add("dve", lambda e, c=c, b2=b2: e.tensor_scalar(
                    out=AO[:, c, :], in0=ps[b2][:], scalar1=vcol(V_PS + l * 2 + c), scalar2=None, op0=ALU.mult),
                    reads=[("ps", b2), "VEC"], writes=[("AO", c)])
            sx = wload(wb["win"][l, 5], 2048, wkey)
            sbb = wload(wb["win"][l, 6], 2048, wkey)
            sc = wload(wb["win"][l, 7], 2048, wkey)
            for c in range(2):
                bx = proj(sx, c)
                bc = proj(sc, c)
                bb = proj(sbb, c)
                P.add("act", lambda e, bx=bx: e.copy(out=PM[1][:, 0:512], in_=ps[bx][:]), reads=[("ps", bx)],
                      writes=[("PM", 1)])
                if t == 0:
                    P.add("dve", lambda e: e.memset(PM[2][:, 0:2], 0.0), writes=[("PM", 2)])
                else:
                    P.add("dve", lambda e, c=c: e.tensor_copy(out=PM[2][:, 0:2], in_=CH[:, c, :]),
                          reads=[("CH", c)], writes=[("PM", 2)])
                P.add("dve", lambda e, bc=bc: e.tensor_tensor(out=PM[2][:, 2:514], in0=ps[bc][:], in1=PM[1][:, 0:512],
                                                              op=ALU.mult),
                      reads=[("ps", bc), ("PM", 1)], writes=[("PM", 2)])
                if t < NT - 1:
                    P.add("dve", lambda e, c=c: e.tensor_copy(out=CH[:, c, :], in_=PM[2][:, 512:514]),
                          reads=[("PM", 2)], writes=[("CH", c)])
                cw = lambda j, c=c: vcol(V_CW + (l * 3 + j) * 2 + c)
                P.add("dve", lambda e, c=c, cw=cw: e.tensor_scalar(
                    out=PM[0][:, 0:512], in0=PM[2][:, 2:514], scalar1=cw(2), scalar2=vcol(V_CB + l * 2 + c),
                    op0=ALU.mult, op1=ALU.add),
                    reads=[("PM", 2), "VEC"], writes=[("PM", 0)])
                P.add("dve", lambda e: e.memset(DUM[:], 0.0), writes=["DUM"])
                P.add("dve", lambda e, cw=cw: e.scalar_tensor_tensor(
                    out=PM[1][:, 0:512], in0=PM[2][:, 1:513], scalar=cw(1), in1=PM[0][:, 0:512], op0=ALU.mult, op1=ALU.add),
                    reads=[("PM", 2), ("PM", 0), "VEC"], writes=[("PM", 1)])
                P.add("dve", lambda e: e.memset(DUM[:], 0.0), writes=["DUM"])
                P.add("dve", lambda e, cw=cw: e.scalar_tensor_tensor(
                    out=PM[0][:, 0:512], in0=PM[2][:, 0:512], scalar=cw(0), in1=PM[1][:, 0:512], op0=ALU.mult, op1=ALU.add),
                    reads=[("PM", 2), ("PM", 1), "VEC"], writes=[("PM", 0)])
                P.add("dve", lambda e: e.memset(DUM[:], 0.0), writes=["DUM"])
                P.add("dve", lambda e, c=c, bb=bb: e.tensor_tensor(out=CO[:, c, :], in0=ps[bb][:], in1=PM[0][:, 0:512],
                                                                    op=ALU.mult),
                      reads=[("ps", bb), ("PM", 0)], writes=[("CO", c)])
            nkb = 4 * (t + 1)
            for c in range(4):
                for hd in range(2):
                    P.add("pool", lambda e, hd=hd: e.memset(R32[hd][:], 0.0), writes=[("R32", hd)])
                kbs = list(reversed(range(nkb)))
                slots = {}

                def geom(kb):
                    j = kb - 4 * t
                    return j, (128 * j if j > 0 else 0)

                def zbank(kb, hd):
                    return hd * 2 + (nkb - 1 - kb) % 2

                def st_z(kb, c=c):
                    j, c0 = geom(kb)
                    kt_ap = KT_(c, kb * 128, (kb + 1) * 128)
                    for hd in range(2):
                        hb = 64 * hd
                        zb = zbank(kb, hd)
                        P.add("pe", lambda e, zb=zb, c0=c0, kt_ap=kt_ap, hb=hb: e.matmul(
                            ps[zb][:, c0:T], lhsT=kt_ap[hb:hb + 64, :], rhs=QS_(c)[hb:hb + 64, c0:T], start=True, stop=True),
                            reads=[("KT", c, kb // 4), ("QS", c)], writes=[("ps", zb)])

                def st_sp(kb):
                    j, c0 = geom(kb)
                    for hd in range(2):
                        zb = zbank(kb, hd)
                        ke = rot_sg.next()
                        ks = rot_asp.next()
                        slots[(kb, hd)] = ks
                        P.add("act", lambda e, zb=zb, c0=c0, ke=ke: e.activation(out=SG[ke][:, c0:T], in_=ps[zb][:, c0:T],
                                                                                  func=AF.Exp),
                              reads=[("ps", zb)], writes=[("SG", ke)])
                        P.add("act", lambda e, c0=c0, ke=ke, ks=ks: e.activation(out=ASP[ks][:, c0:T], in_=SG[ke][:, c0:T],
                                                                                  func=AF.Ln, bias=1.0),
                              reads=[("SG", ke)], writes=[("ASP", ks)])
                        if j >= 0:
                            P.add("dve", lambda e, c0=c0, ks=ks: e.tensor_tensor(
                                out=ASP[ks][:, c0:c0 + 128], in0=ASP[ks][:, c0:c0 + 128], in1=MSK, op=ALU.mult),
                                reads=[("ASP", ks), "CB"], writes=[("ASP", ks)])

                def st_c(kb, first, c=c):
                    j, c0 = geom(kb)
                    for hd in range(2):
                        zb = zbank(kb, hd)
                        ks = slots[(kb, hd)]
                        P.add("pe", lambda e, zb=zb, c0=c0, ks=ks: e.matmul(
                            ps[zb][:, c0:T], lhsT=NTRI, rhs=ASP[ks][:, c0:T], start=False, stop=first,
                            skip_group_check=True),
                            reads=[("ASP", ks), "CB"], writes=[("ps", zb)])
                        if not first:
                            rbi = 2 * hd + (kb + 1) % 2
                            P.add("pe", lambda e, zb=zb, c0=c0, rbi=rbi: e.matmul(
                                ps[zb][:, c0:T], lhsT=ONES1[:], rhs=RB[rbi][:, c0:T], start=False, stop=True,
                                skip_group_check=True),
                                reads=[("RB", rbi), "ONES1"], writes=[("ps", zb)])

                def st_w(kb):
                    j, c0 = geom(kb)
                    for hd in range(2):
                        zb = zbank(kb, hd)
                        kw = rot_aw.next()
                        slots[(kb, hd, "w")] = kw
                        P.add("act", lambda e, zb=zb, c0=c0, kw=kw: e.activation(out=AW[kw][:, c0:T], in_=ps[zb][:, c0:T],
                                                                                  func=AF.Exp),
                              reads=[("ps", zb)], writes=[("AW", kw)])
                        if j >= 0:
                            P.add("dve", lambda e, c0=c0, kw=kw: e.tensor_tensor(
                                out=AW[kw][:, c0:c0 + 128], in0=AW[kw][:, c0:c0 + 128], in1=MSK, op=ALU.mult),
                                reads=[("AW", kw), "CB"], writes=[("AW", kw)])

                def st_av(kb, first, c=c):
                    j, c0 = geom(kb)
                    for hd in range(2):
                        ob = 4 + hd
                        kw = slots[(kb, hd, "w")]
                        P.add("pe", lambda e, ob=ob, c0=c0, kw=kw: e.matmul(
                            ps[ob][:, c0:T], lhsT=V_(kb)[:, c * 128:(c + 1) * 128], rhs=AW[kw][:, c0:T], start=first,
                            stop=(kb == 0), skip_group_check=True),
                            reads=[("V", kb), ("AW", kw)], writes=[("ps", ob)])

                def st_r(kb):
                    j, c0 = geom(kb)
                    jn, c0n = geom(kb - 1)
                    for hd in range(2):
                        ks = slots[(kb, hd)]
                        P.add("pool", lambda e, c0=c0, ks=ks, hd=hd: e.tensor_tensor(
                            out=R32[hd][:, c0:T], in0=R32[hd][:, c0:T], in1=ASP[ks][:, c0:T], op=ALU.add),
                            reads=[("R32", hd), ("ASP", ks)], writes=[("R32", hd)])
                    for hd in range(2):
                        rbi = 2 * hd + kb % 2
                        P.add("dve", lambda e, c0n=c0n, hd=hd, rbi=rbi: e.tensor_copy(out=RB[rbi][:, c0n:T], in_=R32[hd][:, c0n:T]),
                              reads=[("R32", hd)], writes=[("RB", rbi)])

                nb = len(kbs)
                st_z(kbs[0])
                st_sp(kbs[0])
                if kbs[0] > 0:
                    st_r(kbs[0])
                st_c(kbs[0], True)
                if nb > 1:
                    st_z(kbs[1])
                for i, kb in enumerate(kbs):
                    if i + 1 < nb:
                        st_sp(kbs[i + 1])
                        if kbs[i + 1] > 0:
                            st_r(kbs[i + 1])
                        st_c(kbs[i + 1], False)
                    st_w(kb)
                    if i + 2 < nb:
                        st_z(kbs[i + 2])
                    st_av(kb, i == 0)
                    for _ in range(NDUMMY):
                        P.add("pe", lambda e: e.matmul(ps[DUMBANK][:], lhsT=ONESM[:], rhs=PD[:], start=True, stop=True),
                              writes=[("ps", DUMBANK)])
                for hd in range(2):
                    hb = 64 * hd
                    P.add("dve", lambda e, hd=hd, hb=hb, c=c: e.tensor_copy(out=BO[hb:hb + 64, c, :], in_=ps[4 + hd][hb:hb + 64, :]),
                          reads=[("ps", 4 + hd)], writes=[("BO", c)])
            for n in range(DC):
                sA = wload(wb["wgb"][l, n, 0], 2048, ("W", "wgb", l))
                sB = wload(wb["wgb"][l, n, 1], 2048, ("W", "wgb", l))
                gb = [rot_p.next() for _ in range(3)]
                bb_ = [rot_p.next() for _ in range(3)]
                for gi in range(3):
                    sl, off = (sA, gi * 8) if gi < 2 else (sB, 0)
                    for kc in range(DC):
                        P.add("pe", lambda e, kc=kc, sl=sl, off=off, bk=gb[gi]: e.matmul(
                            ps[bk][:], lhsT=WB[sl][:, (off + kc) * 128:(off + kc + 1) * 128], rhs=rhsHN(kc),
                            start=(kc == 0), stop=(kc == DC - 1)),
                            reads=[("WB", sl), ("HN", kc, hh)], writes=[("ps", gb[gi])])
                for bi, (off, nk, src, key) in enumerate(((8, 2, AO, "AO"), (10, 4, BO, "BO"), (14, 2, CO, "CO"))):
                    for kc in range(nk):
                        P.add("pe", lambda e, kc=kc, off=off, nk=nk, src=src, bk=bb_[bi], sB=sB: e.matmul(
                            ps[bk][:], lhsT=WB[sB][:, (off + kc) * 128:(off + kc + 1) * 128], rhs=src[:, kc, :],
                            start=(kc == 0), stop=(kc == nk - 1)),
                            reads=[("WB", sB), (key, kc)], writes=[("ps", bb_[bi])])
                ks = [rot_sg.next() for _ in range(3)]
                for gi in range(3):
                    P.add("act", lambda e, gi=gi, k=ks[gi], bk=gb[gi], n=n: e.activation(
                        out=SG[k][:], in_=ps[bk][:], func=AF.Sigmoid, bias=vcol(V_BG + (l * 3 + gi) * 8 + n)),
                        reads=[("ps", gb[gi]), "VEC"], writes=[("SG", ks[gi])])
                for gi in range(3):
                    P.add("dve", lambda e, k=ks[gi], bk=bb_[gi]: e.tensor_tensor(
                        out=SG[k][:], in0=ps[bk][:], in1=SG[k][:], op=ALU.mult),
                        reads=[("ps", bb_[gi]), ("SG", ks[gi])], writes=[("SG", ks[gi])])
                P.add("pool", lambda e, k0=ks[0], k1=ks[1]: e.tensor_tensor(out=SG[k0][:], in0=SG[k0][:], in1=SG[k1][:],
                                                                             op=ALU.add),
                      reads=[("SG", ks[0]), ("SG", ks[1])], writes=[("SG", ks[0])])
                P.add("pool", lambda e, k0=ks[0], k2=ks[2], n=n: e.tensor_tensor(
                    out=HN[:, n, mh * T:(mh + 1) * T], in0=SG[k0][:], in1=SG[k2][:], op=ALU.add),
                    reads=[("SG", ks[0]), ("SG", ks[2])], writes=[("HN", n, mh)])
            pend = None
            for i in range(4):
                s = wload(wb["wo"][l, i], 2048, ("W", "wo", l))
                for j in range(2):
                    n = 2 * i + j
                    bk = rot_p.next()
                    for kc in range(DC):
                        P.add("pe", lambda e, kc=kc, s=s, j=j, bk=bk: e.matmul(
                            ps[bk][:], lhsT=WB[s][:, (j * 8 + kc) * 128:(j * 8 + kc + 1) * 128],
                            rhs=HN[:, kc, mh * T:(mh + 1) * T], start=(kc == 0), stop=(kc == DC - 1)),
                            reads=[("WB", s), ("HN", kc, mh)], writes=[("ps", bk)])
                    if pend is not None:
                        pend()
                    pend = evac_h_and_stats(bk, n, None)
            pend()
            postnorm_update(V_G + (3 * L + l) * 8, t, 1.0)

        def x_load(s_, hf):
            for c in range(DC):
                P.add("sp", lambda e, c=c: e.dma_start(out=X[:, c, hf * 1024:(hf + 1) * 1024],
                                                       in_=xT[s_, c][:, hf * 1024:(hf + 1) * 1024]),
                      reads=(WKEYS if (s_ == 0 and hf == 0) else []),
                      writes=[("X", c, 2 * hf), ("X", c, 2 * hf + 1)], dma_key="X%d_%d" % (c, hf))

        def x_store(s_, hf):
            for c in range(DC):
                P.add("sp", lambda e, c=c: e.dma_start(out=yT[s_, c][:, hf * 1024:(hf + 1) * 1024],
                                                       in_=X[:, c, hf * 1024:(hf + 1) * 1024]),
                      reads=[("X", c, 2 * hf), ("X", c, 2 * hf + 1)], writes=[("Y", s_, c, hf)],
                      dma_key="X%d_%d" % (c, hf))

        x_load(0, 0)
        x_load(0, 1)
        for s_ in range(NS):
            for l in range(NL):
                last = (l == NL - 1)
                if "ffn1" in parts:
                    ffn(l, 0, 0)
                    ffn(l, 0, 1)
                fence()
                if "mix" in parts:
                    for t in range(NT):
                        mix(l, t)
                fence()
                if "ffn2" in parts:
                    ffn(l, 1, 0)
                if last:
                    x_store(s_, 0)
                    if s_ + 1 < NS:
                        x_load(s_ + 1, 0)
                if "ffn2" in parts:
                    ffn(l, 1, 1)
            x_store(s_, 1)
            if s_ + 1 < NS:
                x_load(s_ + 1, 1)
        P.add("sp", lambda e: None, reads=[("Y", s_, c, hf) for s_ in range(NS) for c in range(DC) for hf in range(2)])
        P.emit(nc, st)
    return nc, len(P.ops)


def _kcp(w):
    K, M = w.shape
    return w.reshape(K // 128, 128, M // 128, 128).transpose(2, 1, 0, 3)


def pack_weights(inp):
    f = np.float32
    W = {}
    wgu = np.empty((L, 2, FC, 128, 16, 128), f)
    wd = np.empty((L, 2, DC, 2, 128, 11, 128), f)
    win = np.empty((L, 8, 128, 2, 8, 128), f)
    wv = np.empty((L, 2, 128, 4, 512), f)
    wgb = np.empty((L, DC, 2, 128, 16, 128), f)
    wo = np.empty((L, 4, 128, 2, 8, 128), f)
    cols = np.concatenate([np.arange(768, 1280), np.arange(256, 768), np.arange(0, 256), np.arange(1792, 2560)])
    for l in range(L):
        for fi, pre in enumerate(("ffn1", "ffn2")):
            wgu[l, fi, :, :, 0:8] = _kcp(np.asarray(inp[pre + "_w_gate"][l]))
            wgu[l, fi, :, :, 8:16] = _kcp(np.asarray(inp[pre + "_w_up"][l]))
            d = _kcp(np.asarray(inp[pre + "_w_down"][l]))
            wd[l, fi] = d.reshape(DC, 128, 2, 11, 128).transpose(0, 2, 1, 3, 4)
        wi = np.asarray(inp["w_in"][l])
        p = _kcp(wi[:, cols])
        win[l] = p.reshape(8, 2, 128, 8, 128).transpose(0, 2, 1, 3, 4)
        v = wi[:, 1280:1792].reshape(2, 4, 128, 512)
        wv[l] = v.transpose(0, 2, 1, 3)
        g = [_kcp(wi[:, 2560 + i * 1024: 2560 + (i + 1) * 1024]) for i in range(3)]
        wgb[l, :, 0, :, 0:8] = g[0]
        wgb[l, :, 0, :, 8:16] = g[1]
        wgb[l, :, 1, :, 0:8] = g[2]
        wgb[l, :, 1, :, 8:10] = _kcp(np.asarray(inp["w_br_pool"][l]))
        wgb[l, :, 1, :, 10:14] = _kcp(np.asarray(inp["w_br_sb"][l]))
        wgb[l, :, 1, :, 14:16] = _kcp(np.asarray(inp["w_br_conv"][l]))
        o = _kcp(np.asarray(inp["w_out"][l]))
        wo[l] = o.reshape(4, 2, 128, 8, 128).transpose(0, 2, 1, 3, 4)
    W["wgu"] = wgu.reshape(L, 2, FC, 128, 2048)
    W["wd"] = wd.reshape(L, 2, DC, 2, 128, 1408)
    W["win"] = win.reshape(L, 8, 128, 2048)
    W["wv"] = wv.reshape(L, 2, 128, 2048)
    W["wgb"] = wgb.reshape(L, DC, 2, 128, 2048)
    W["wo"] = wo.reshape(L, 4, 128, 2048)
    vec = np.zeros((128, NV), f)
    gn = ("ffn1_pre_g", "ffn1_post_g", "mix_pre_g", "mix_post_g", "ffn2_pre_g", "ffn2_post_g")
    for gi, name in enumerate(gn):
        g_ = np.asarray(inp[name])
        vec[:, V_G + gi * L * 8: V_G + (gi + 1) * L * 8] = g_.reshape(L, 8, 128).transpose(2, 0, 1).reshape(128, L * 8)
    bg = np.asarray(inp["b_gate"])
    vec[:, V_BG:V_BG + L * 3 * 8] = bg.reshape(L, 3, 8, 128).transpose(3, 0, 1, 2).reshape(128, L * 24)
    vec[:, V_PS:V_PS + L * 2] = np.asarray(inp["pool_scale"]).reshape(L, 2, 128).transpose(2, 0, 1).reshape(128, L * 2)
    vec[:, V_CW:V_CW + L * 6] = np.asarray(inp["conv_w"]).reshape(L, 3, 2, 128).transpose(3, 0, 1, 2).reshape(128, L * 6)
    vec[:, V_CB:V_CB + L * 2] = np.asarray(inp["conv_b"]).reshape(L, 2, 128).transpose(2, 0, 1).reshape(128, L * 2)
    wins = (2, 4, 8, 16)
    for c in range(2):
        for hlf in range(2):
            w_ = wins[2 * c + hlf]
            vec[64 * hlf:64 * hlf + 64, V_IW + c] = 1.0 / w_
            for t in range(16):
                vec[64 * hlf:64 * hlf + 64, V_IC + c * 16 + t] = 1.0 / min(t + 1, w_)
    W["vec"] = vec
    cst = np.zeros((128, 384), f)
    i = np.arange(128)
    cst[:, 0:128] = (i[:, None] >= i[None, :]).astype(f)
    cst[:, 128:256] = (i[None, :] > i[:, None]).astype(f)
    cst[:, 256:384] = -cst[:, 0:128]
    W["cst"] = cst
    pw = np.zeros((128, L, 2, 128), f)
    pool_w = np.asarray(inp["pool_w"])
    for l in range(L):
        for c in range(2):
            for hlf in range(2):
                pw[64 * hlf:64 * hlf + 64, l, c, 64 * hlf:64 * hlf + 64] = pool_w[l, 2 * c + hlf]
    W["poolw"] = pw.reshape(128, L * 2 * 128)
    return W


_NC_CACHE = {}


def kernel(**inputs):
    x = np.asarray(inputs["x"], dtype=np.float32)
    B = x.shape[0]
    W = pack_weights(inputs)
    key = (SEQ_PER_CORE, L)
    if key not in _NC_CACHE:
        _NC_CACHE[key] = build_program(SEQ_PER_CORE, L)[0]
    nc = _NC_CACHE[key]
    in_maps = []
    for c in range(NCORE):
        xs = x[c * SEQ_PER_CORE:(c + 1) * SEQ_PER_CORE]
        xT = np.ascontiguousarray(xs.transpose(0, 2, 1)).reshape(SEQ_PER_CORE, DC, 128, S)
        m = {"xT": xT}
        m.update(W)
        in_maps.append(m)
    res = run_bass_kernel_spmd(nc, in_maps, core_ids=list(range(NCORE)))
    out = np.empty((B, S, D), np.float32)
    for c in range(NCORE):
        yT = np.asarray(res.results[c]["yT"]).reshape(SEQ_PER_CORE, D, S)
        out[c * SEQ_PER_CORE:(c + 1) * SEQ_PER_CORE] = yT.transpose(0, 2, 1)
    return out
```
